# Optimizing a Trainium2 kernel written in Bass

```python
import math
import jax, jax.numpy as jnp
from jax import lax
import numpy as np

D_MODEL = 1024
BATCH = 4
SEQ = 8192
DEPTH = 4

SSM_GROUP = 16
D_SSM = D_MODEL // 2
SSM_GROUPS = D_SSM // SSM_GROUP
SSM_STATE = 64
DT_MIN = 0.001
DT_MAX = 0.1
D_CONV = D_MODEL // 2
CONV_WIDTH = 31
ATTN_HEADS = 8
ATTN_HEAD_DIM = 64
ATTN_MAPS = 2 * ATTN_HEADS
D_QK = ATTN_MAPS * ATTN_HEAD_DIM
D_V = ATTN_HEADS * 2 * ATTN_HEAD_DIM
Q_BLOCK = 128
REL_BUCKETS = 32
REL_MAX_DIST = 128
D_FF = -(-8 * D_MODEL // (3 * 256)) * 256
N_BRANCH = 3
OFF_SSM = 0
OFF_CONV = OFF_SSM + D_SSM
OFF_Q = OFF_CONV + 2 * D_CONV
OFF_K = OFF_Q + D_QK
OFF_V = OFF_K + D_QK
OFF_GATE = OFF_V + D_V
D_IN = OFF_GATE + N_BRANCH * D_MODEL

RMS_EPS = 1e-6
SUBLN_EPS = 1e-5
LN_EPS = 1e-5

kernel_name = "hybrid_s5_conformer_diffattn_block"


def rms_norm(x, g, eps=RMS_EPS):
    xf = x.astype(jnp.float32)
    y = xf * lax.rsqrt(jnp.mean(xf * xf, axis=-1, keepdims=True) + eps)
    return (y * g.astype(jnp.float32)).astype(x.dtype)


def layer_norm(x, g, b, eps=LN_EPS):
    xf = x.astype(jnp.float32)
    mu = jnp.mean(xf, axis=-1, keepdims=True)
    var = jnp.mean(jnp.square(xf - mu), axis=-1, keepdims=True)
    y = (xf - mu) * lax.rsqrt(var + eps)
    return (y * g.astype(jnp.float32) + b.astype(jnp.float32)).astype(x.dtype)


def _scan_op(e1, e2):
    a1, b1 = e1
    a2, b2 = e2
    return a1 * a2, a2 * b1 + b2


def s5_branch(u, a_re, a_im, log_dt, b_re, b_im, c_re, c_im, d_skip, w_glu):
    bsz, length, _ = u.shape
    uf = u.astype(jnp.float32).reshape(bsz, length, SSM_GROUPS, SSM_GROUP)
    lam = lax.complex(a_re.astype(jnp.float32), a_im.astype(jnp.float32))
    dt = jnp.exp(log_dt.astype(jnp.float32))[:, None]
    lam_bar = jnp.exp(lam * dt)
    b_mat = lax.complex(b_re.astype(jnp.float32), b_im.astype(jnp.float32))
    b_bar = ((lam_bar - 1.0) / lam)[..., None] * b_mat
    bu = jnp.einsum("blgc,gpc->blgp", uf.astype(jnp.complex64), b_bar)
    a = jnp.broadcast_to(lam_bar, bu.shape)
    _, states = lax.associative_scan(_scan_op, (a, bu), axis=1)
    c_mat = lax.complex(c_re.astype(jnp.float32), c_im.astype(jnp.float32))
    y = jnp.real(jnp.einsum("blgp,gcp->blgc", states, c_mat)) + d_skip.astype(jnp.float32) * uf
    y = jax.nn.gelu(y.reshape(bsz, length, D_SSM)).astype(u.dtype)
    z = y @ w_glu
    return z[..., :D_SSM] * jax.nn.sigmoid(z[..., D_SSM:])


def conformer_conv(c_in, dw, dw_b, ln_g, ln_b, w_out):
    a, b = jnp.split(c_in, 2, axis=-1)
    g = a * jax.nn.sigmoid(b)
    y = lax.conv_general_dilated(
        g, dw[:, None, :], window_strides=(1,), padding=[(CONV_WIDTH - 1, 0)],
        dimension_numbers=("NWC", "WIO", "NWC"), feature_group_count=D_CONV) + dw_b
    y = jax.nn.silu(layer_norm(y, ln_g, ln_b))
    return y @ w_out


def rel_bucket(rel):
    n = jnp.maximum(rel, 0)
    max_exact = REL_BUCKETS // 2
    nf = jnp.maximum(n, max_exact).astype(jnp.float32)
    large = max_exact + (jnp.log(nf / max_exact) / math.log(REL_MAX_DIST / max_exact)
                         * (REL_BUCKETS - max_exact)).astype(jnp.int32)
    large = jnp.minimum(large, REL_BUCKETS - 1)
    return jnp.where(n < max_exact, n, large)


def diff_attention(q, k, v, rel_bias, lam, subln_g, lambda_init):
    bsz, length, _ = q.shape
    q = q.reshape(bsz, length, ATTN_MAPS, ATTN_HEAD_DIM)
    k = k.reshape(bsz, length, ATTN_MAPS, ATTN_HEAD_DIM)
    v = v.reshape(bsz, length, ATTN_HEADS, 2 * ATTN_HEAD_DIM)
    n_blk = length // Q_BLOCK
    qb = q.reshape(bsz, n_blk, Q_BLOCK, ATTN_MAPS, ATTN_HEAD_DIM).transpose(1, 0, 2, 3, 4)
    k_pos = jnp.arange(length)
    scale = ATTN_HEAD_DIM ** -0.5

    def one_block(args):
        i, qi = args
        q_pos = i * Q_BLOCK + jnp.arange(Q_BLOCK)
        rel = q_pos[:, None] - k_pos[None, :]
        bias = rel_bias[rel_bucket(rel)].astype(jnp.float32).transpose(2, 0, 1)
        s = jnp.einsum("bqmd,bkmd->bmqk", qi, k).astype(jnp.float32) * scale + bias
        s = jnp.where(rel >= 0, s, -jnp.inf)
        p = jax.nn.softmax(s, axis=-1).reshape(bsz, ATTN_HEADS, 2, Q_BLOCK, length)
        attn = p[:, :, 0] - lam * p[:, :, 1]
        return jnp.einsum("bhqk,bkhe->bqhe", attn.astype(v.dtype), v)

    o = lax.map(one_block, (jnp.arange(n_blk), qb))
    o = o.transpose(1, 0, 2, 3, 4).reshape(bsz, length, ATTN_HEADS, 2 * ATTN_HEAD_DIM)
    o = rms_norm(o, subln_g, SUBLN_EPS) * (1.0 - lambda_init)
    return o.reshape(bsz, length, D_V)


def setup_inputs(seed: int = 0) -> dict:
    key = jax.random.key(seed)
    ks = jax.random.split(key, 32)
    f32 = jnp.float32

    def nrm(k, shape, scale):
        return jax.random.normal(k, shape, f32) * scale

    L = DEPTH
    a_im_base = math.pi * jnp.arange(SSM_STATE, dtype=f32)
    return {
        "x": nrm(ks[0], (BATCH, SEQ, D_MODEL), 1.0),
        "rel_bias": nrm(ks[1], (REL_BUCKETS, ATTN_MAPS), 0.5),
        "pre_mix_g": 1.0 + nrm(ks[2], (L, D_MODEL), 0.01),
        "w_in": nrm(ks[3], (L, D_MODEL, D_IN), D_MODEL ** -0.5),
        "ssm_a_re": -0.5 + nrm(ks[4], (L, SSM_GROUPS, SSM_STATE), 0.01),
        "ssm_a_im": a_im_base + nrm(ks[5], (L, SSM_GROUPS, SSM_STATE), 0.01),
        "ssm_log_dt": jax.random.uniform(ks[6], (L, SSM_GROUPS), f32, math.log(DT_MIN), math.log(DT_MAX)),
        "ssm_b_re": nrm(ks[7], (L, SSM_GROUPS, SSM_STATE, SSM_GROUP), (2 * SSM_GROUP) ** -0.5),
        "ssm_b_im": nrm(ks[8], (L, SSM_GROUPS, SSM_STATE, SSM_GROUP), (2 * SSM_GROUP) ** -0.5),
        "ssm_c_re": nrm(ks[9], (L, SSM_GROUPS, SSM_GROUP, SSM_STATE), (2 * SSM_STATE) ** -0.5),
        "ssm_c_im": nrm(ks[10], (L, SSM_GROUPS, SSM_GROUP, SSM_STATE), (2 * SSM_STATE) ** -0.5),
        "ssm_d": nrm(ks[11], (L, SSM_GROUPS, SSM_GROUP), 1.0),
        "w_ssm_glu": nrm(ks[12], (L, D_SSM, 2 * D_SSM), D_SSM ** -0.5),
        "w_ssm_out": nrm(ks[13], (L, D_SSM, D_MODEL), D_SSM ** -0.5),
        "conv_dw": nrm(ks[14], (L, CONV_WIDTH, D_CONV), CONV_WIDTH ** -0.5),
        "conv_dw_b": nrm(ks[15], (L, D_CONV), 0.01),
        "conv_ln_g": 1.0 + nrm(ks[16], (L, D_CONV), 0.01),
        "conv_ln_b": nrm(ks[17], (L, D_CONV), 0.01),
        "w_conv_out": nrm(ks[18], (L, D_CONV, D_MODEL), D_CONV ** -0.5),
        "lambda_q1": nrm(ks[19], (L, ATTN_HEAD_DIM), 0.1),
        "lambda_k1": nrm(ks[20], (L, ATTN_HEAD_DIM), 0.1),
        "lambda_q2": nrm(ks[21], (L, ATTN_HEAD_DIM), 0.1),
        "lambda_k2": nrm(ks[22], (L, ATTN_HEAD_DIM), 0.1),
        "attn_subln_g": 1.0 + nrm(ks[23], (L, 2 * ATTN_HEAD_DIM), 0.01),
        "w_attn_out": nrm(ks[24], (L, D_V, D_MODEL), D_V ** -0.5),
        "w_out": nrm(ks[25], (L, D_MODEL, D_MODEL), D_MODEL ** -0.5),
        "post_mix_g": 1.0 + nrm(ks[26], (L, D_MODEL), 0.01),
        "pre_ffn_g": 1.0 + nrm(ks[27], (L, D_MODEL), 0.01),
        "w_ffn_in": nrm(ks[28], (L, D_MODEL, 2 * D_FF), D_MODEL ** -0.5),
        "w_ffn_out": nrm(ks[29], (L, D_FF, D_MODEL), D_FF ** -0.5),
        "post_ffn_g": 1.0 + nrm(ks[30], (L, D_MODEL), 0.01),
    }


def reference(x, rel_bias, pre_mix_g, w_in, ssm_a_re, ssm_a_im, ssm_log_dt, ssm_b_re, ssm_b_im,
              ssm_c_re, ssm_c_im, ssm_d, w_ssm_glu, w_ssm_out, conv_dw, conv_dw_b, conv_ln_g,
              conv_ln_b, w_conv_out, lambda_q1, lambda_k1, lambda_q2, lambda_k2, attn_subln_g,
              w_attn_out, w_out, post_mix_g, pre_ffn_g, w_ffn_in, w_ffn_out, post_ffn_g):
    bsz, length, _ = x.shape
    for layer in range(DEPTH):
        lambda_init = 0.8 - 0.6 * math.exp(-0.3 * layer)
        h = rms_norm(x, pre_mix_g[layer])
        proj = h @ w_in[layer]
        u_ssm = proj[..., OFF_SSM:OFF_CONV]
        c_in = proj[..., OFF_CONV:OFF_Q]
        q = proj[..., OFF_Q:OFF_K]
        k = proj[..., OFF_K:OFF_V]
        v = proj[..., OFF_V:OFF_GATE]
        gates = jax.nn.sigmoid(proj[..., OFF_GATE:]).reshape(bsz, length, N_BRANCH, D_MODEL)

        y_a = s5_branch(u_ssm, ssm_a_re[layer], ssm_a_im[layer], ssm_log_dt[layer],
                        ssm_b_re[layer], ssm_b_im[layer], ssm_c_re[layer], ssm_c_im[layer],
                        ssm_d[layer], w_ssm_glu[layer]) @ w_ssm_out[layer]
        y_b = conformer_conv(c_in, conv_dw[layer], conv_dw_b[layer], conv_ln_g[layer],
                             conv_ln_b[layer], w_conv_out[layer])
        lam = (jnp.exp(jnp.sum(lambda_q1[layer].astype(jnp.float32) * lambda_k1[layer].astype(jnp.float32)))
               - jnp.exp(jnp.sum(lambda_q2[layer].astype(jnp.float32) * lambda_k2[layer].astype(jnp.float32)))
               + lambda_init)
        y_c = diff_attention(q, k, v, rel_bias, lam, attn_subln_g[layer], lambda_init) @ w_attn_out[layer]

        m = gates[:, :, 0] * y_a + gates[:, :, 1] * y_b + gates[:, :, 2] * y_c
        x = x + rms_norm(m @ w_out[layer], post_mix_g[layer])
        h = rms_norm(x, pre_ffn_g[layer])
        gu = h @ w_ffn_in[layer]
        f = (jax.nn.silu(gu[..., :D_FF]) * gu[..., D_FF:]) @ w_ffn_out[layer]
        x = x + rms_norm(f, post_ffn_g[layer])
    return x
```

```python
import math
from contextlib import ExitStack
import numpy as np
import concourse.bass as bass
import concourse.mybir as mybir
from concourse.bass_utils import run_bass_kernel_spmd

F32 = mybir.dt.float32
BF16 = mybir.dt.bfloat16
ALU = mybir.AluOpType
AF = mybir.ActivationFunctionType

T = 8192
D = 1024
TT = 512
NT = T // TT
DEPTH = 4
DFF = 2816
NFC = DFF // 128
SAME_ENGINE_SYNC = True


class Buf:
    __slots__ = ("name", "lw", "rd")

    def __init__(self, name):
        self.name = name
        self.lw = None
        self.rd = []


def bufs(name, n):
    return [Buf(f"{name}{i}") for i in range(n)]


class Sched:
    CE = ["act", "pe", "dve", "pool"]

    def __init__(self, nc, n_dma_sems=40):
        self.nc = nc
        self.csem = {e: nc.alloc_semaphore(name=f"c_{e}") for e in self.CE}
        self.ccnt = {e: 0 for e in self.CE}
        self.dsems = [nc.alloc_semaphore(name=f"dm{i}") for i in range(n_dma_sems)]
        self.dcnt = [0] * n_dma_sems
        self.es = None
        self.reset()

    def reset(self):
        self.lists = {e: [] for e in ["sync", "act", "pe", "dve", "pool"]}
        self.seen = {}
        self.dkey = {}
        self.final = {e: {} for e in self.lists}
        self.ninst = 0

    def sb(self, name, shape, dtype):
        self.uid = getattr(self, "uid", 0) + 1
        return self.es.enter_context(self.nc.sbuf_tensor(f"{name}_{self.uid}", shape, dtype))

    def ps(self, name, shape=(128, 512), dtype=F32):
        self.uid = getattr(self, "uid", 0) + 1
        return self.es.enter_context(self.nc.psum_tensor(f"{name}_{self.uid}", list(shape), dtype))

    def add(self, eng, op, kw, reads=(), writes=(), dma=False):
        deps = []
        for b in reads:
            if b.lw is not None:
                deps.append(b.lw)
        for b in writes:
            if b.lw is not None:
                deps.append(b.lw)
            deps.extend(b.rd)
        mykey = None
        if dma:
            kb = (writes[0] if writes else reads[0]).name + ("_w" if writes else "_r")
            if kb not in self.dkey:
                self.dkey[kb] = len(self.dkey)
                assert len(self.dkey) <= len(self.dsems), "out of dma sems"
            mykey = self.dkey[kb]
        waits = []
        for tok in deps:
            kind, key, val = tok
            if kind == "d" and dma and key == mykey:
                continue
            if kind == "c":
                if key == eng and (eng == "pe" or not SAME_ENGINE_SYNC):
                    continue
                sk = (eng, "c", key)
                sem = self.csem[key]
            else:
                sk = (eng, "d", key)
                sem = self.dsems[key]
            if self.seen.get(sk, 0) >= val:
                continue
            self.seen[sk] = val
            waits.append((sem, val))
        if dma:
            di = mykey
            self.dcnt[di] += 16
            tok = ("d", di, self.dcnt[di])
            inc = (self.dsems[di], 16)
            self.final[eng][di] = self.dcnt[di]
        else:
            self.ccnt[eng] += 1
            tok = ("c", eng, self.ccnt[eng])
            inc = (self.csem[eng], 1)
        self.lists[eng].append((waits, (op, kw), inc))
        for b in reads:
            b.rd.append(tok)
        for b in writes:
            b.lw = tok
            b.rd = []
        self.ninst += 1
        return tok

    def flush(self):
        nc = self.nc
        with nc.Block() as block:
            for en, attr in (("sync", block.sync), ("act", block.scalar), ("pe", block.tensor),
                             ("dve", block.vector), ("pool", block.gpsimd)):
                lst = self.lists[en]
                fin = self.final[en]

                def body(eng, lst=lst, fin=fin):
                    for waits, fn, inc in lst:
                        for sem, val in waits:
                            eng.wait_ge(sem, val)
                        ins = getattr(eng, fn[0])(**fn[1])
                        ins.then_inc(inc[0], inc[1])
                    for di, val in fin.items():
                        eng.wait_ge(self.dsems[di], val)

                attr(body)
        self.reset()


def A(S, eng, _opname, reads=(), writes=(), dma=False, **kw):
    return S.add(eng, _opname, kw, reads, writes, dma)


def fm(dram, c0, c1, t0, n=TT):
    return dram.rearrange("(c p) t -> p c t", p=128)[:, c0:c1, t0:t0 + n]


def rms_rstd(S, sq_aps, sq_bufs, ones, ps_ss, ps_ss_b, rs, rs_b, rstd, rstd_b, eps_ap, inv_n):
    n = len(sq_aps)
    for kc in range(n):
        A(S, "pe", "matmul", reads=[sq_bufs[kc]], writes=[ps_ss_b], out=ps_ss, lhsT=ones, rhs=sq_aps[kc], start=(kc == 0), stop=(kc == n - 1))
    A(S, "act", "activation", reads=[ps_ss_b], writes=[rs_b], out=rs, in_=ps_ss, func=AF.Sqrt, bias=eps_ap, scale=inv_n)
    A(S, "dve", "reciprocal", reads=[rs_b], writes=[rstd_b], out=rstd, in_=rs)


def load_w_cast(S, Wt, Wb, src2d, kcn, ncols, col0=0, maxc=2048):
    v = src2d.rearrange("(kc p) n -> p kc n", p=128)
    for kc in range(kcn):
        c = 0
        while c < ncols:
            w = min(maxc, ncols - c)
            A(S, "pool", "dma_start", writes=[Wb], dma=True, out=Wt[:, kc, c:c + w], in_=v[:, kc, col0 + c:col0 + c + w])
            c += w


def load(S, dst_ap, src_ap, b):
    A(S, "sync", "dma_start", writes=[b] if not isinstance(b, list) else b, dma=True, out=dst_ap, in_=src_ap)


def store(S, dst_ap, src_ap, rb):
    A(S, "sync", "dma_start", reads=rb if isinstance(rb, list) else [rb], dma=True, out=dst_ap, in_=src_ap)


class PsRot:
    def __init__(self, S, n, name="ps"):
        self.t = [S.ps(f"{name}{i}") for i in range(n)]
        self.b = bufs(name, n)
        self.i = 0
        self.n = n

    def next(self):
        k = self.i % self.n
        self.i += 1
        return self.t[k], self.b[k]


def mm_group(S, pr, lhs_fn, rhs_fn, kcn, reads_fn):
    pt, pb = pr.next()
    for kc in range(kcn):
        A(S, "pe", "matmul", reads=reads_fn(kc), writes=[pb], out=pt[:], lhsT=lhs_fn(kc), rhs=rhs_fn(kc), start=(kc == 0), stop=(kc == kcn - 1))
    return pt, pb


def phase_p1a(S, L, xsrc, dr, inp, ntiles=NT):
    with ExitStack() as es:
        S.es = es
        W = S.sb("W1a", [128, 8, 3584], BF16); Wb = Buf("W1a")
        xt = [S.sb(f"x{i}", [128, 8, TT], F32) for i in range(2)]; xb = bufs("x", 2)
        sq = S.sb("sq", [128, 8, TT], BF16); sqb = bufs("sq", 8)
        h = [S.sb(f"h{i}", [128, 8, TT], BF16) for i in range(2)]; hb = [bufs(f"h{i}_", 8) for i in range(2)]
        rs = S.sb("rs", [128, TT], F32); rsb = Buf("rs")
        rstd = S.sb("rstd", [128, TT], F32); rstdb = Buf("rstd")
        gcol = S.sb("gcol", [128, 8], F32); gcb = Buf("gcol")
        ones = S.sb("ones", [128, 128], BF16); onesb = Buf("ones")
        epsT = S.sb("eps", [128, 1], F32); epsb = Buf("eps")
        uo = S.sb("uo", [128, 4, TT], BF16); uob = bufs("uo", 4)
        sg = S.sb("sg", [128, 4, TT], BF16); sgb = bufs("sg", 4)
        go = S.sb("go", [128, 4, TT], BF16); gob = bufs("go", 4)
        qo = S.sb("qo", [128, 8, TT], BF16); qob = bufs("qo", 8)
        ko = S.sb("ko", [128, 8, TT], BF16); kob = bufs("ko", 8)
        pr = PsRot(S, 6)
        pss = S.ps("pss"); pssb = Buf("pss")

        A(S, "dve", "memset", writes=[onesb], ap=ones[:], constant=1.0)
        A(S, "dve", "memset", writes=[epsb], ap=epsT[:], constant=1e-6)
        load(S, gcol[:], inp["g_pre_mix"][L], gcb)
        load_w_cast(S, W, Wb, inp["w_in"][L], 8, 3584, 0, maxc=1792)

        load(S, xt[0][:], fm(xsrc, 0, 8, 0), xb[0])
        for tt in range(ntiles):
            s = tt % 2
            t0 = tt * TT
            if tt + 1 < ntiles:
                load(S, xt[1 - s][:], fm(xsrc, 0, 8, t0 + TT), xb[1 - s])
            for kc in range(8):
                A(S, "act", "activation", reads=[xb[s]], writes=[sqb[kc]], out=sq[:, kc, :], in_=xt[s][:, kc, :], func=AF.Square)
            rms_rstd(S, [sq[:, kc, :] for kc in range(8)], sqb, ones[:], pss[:], pssb, rs[:], rsb, rstd[:], rstdb, epsT[:, 0:1], 1.0 / D)
            for kc in range(8):
                A(S, "dve", "scalar_tensor_tensor", reads=[xb[s], gcb, rstdb], writes=[hb[s][kc]], out=h[s][:, kc, :], in0=xt[s][:, kc, :],
                  scalar=gcol[:, kc:kc + 1], in1=rstd[:], op0=ALU.mult, op1=ALU.mult)
            store(S, fm(dr["hT"], 0, 8, t0), h[s][:], hb[s])

            def group(cc):
                return mm_group(S, pr, lambda kc: W[:, kc, cc * 128:(cc + 1) * 128], lambda kc: h[s][:, kc, :], 8, lambda kc: [Wb, hb[s][kc]])

            for c in range(4):
                pt, pb = group(c)
                A(S, "dve", "tensor_copy", reads=[pb], writes=[uob[c]], out=uo[:, c, :], in_=pt[:])
            store(S, fm(dr["uT"], 0, 4, t0), uo[:], uob)
            for c in range(4):
                pt, pb = group(8 + c)
                A(S, "act", "activation", reads=[pb], writes=[sgb[c]], out=sg[:, c, :], in_=pt[:], func=AF.Sigmoid)
            for c in range(4):
                pt, pb = group(4 + c)
                A(S, "dve", "tensor_tensor", reads=[pb, sgb[c]], writes=[gob[c]], out=go[:, c, :], in0=pt[:], in1=sg[:, c, :], op=ALU.mult)
            store(S, fm(dr["gT"], 0, 4, t0), go[:], gob)
            for c in range(8):
                pt, pb = group(12 + c)
                A(S, "act", "mul", reads=[pb], writes=[qob[c]], out=qo[:, c, :], in_=pt[:], mul=0.125)
            store(S, fm(dr["qT"], 0, 8, t0), qo[:], qob)
            for c in range(8):
                pt, pb = group(20 + c)
                A(S, "dve", "tensor_copy", reads=[pb], writes=[kob[c]], out=ko[:, c, :], in_=pt[:])
            store(S, fm(dr["kT"], 0, 8, t0), ko[:], kob)
        S.flush()


def phase_p1b(S, L, dr, inp, ntiles=NT):
    with ExitStack() as es:
        S.es = es
        W = S.sb("W1b", [128, 8, 4096], BF16); Wb = Buf("W1b")
        h = [S.sb(f"h{i}", [128, 8, TT], BF16) for i in range(2)]; hb = bufs("h", 2)
        vo = S.sb("vo", [128, 4, 1024], BF16); vob = bufs("vo", 8)
        go = [S.sb(f"go{i}", [128, 8, TT], BF16) for i in range(2)]; gob = [bufs(f"go{i}_", 8) for i in range(2)]
        pr = PsRot(S, 7)
        load_w_cast(S, W, Wb, inp["w_in"][L], 8, 4096, 3584)
        load(S, h[0][:], fm(dr["hT"], 0, 8, 0), hb[0])
        for tt in range(ntiles):
            s = tt % 2
            t0 = tt * TT
            if tt + 1 < ntiles:
                load(S, h[1 - s][:], fm(dr["hT"], 0, 8, t0 + TT), hb[1 - s])
            for tb in range(4):
                for ch in range(2):
                    pt, pb = mm_group(S, pr, lambda kc: h[s][:, kc, tb * 128:(tb + 1) * 128], lambda kc: W[:, kc, ch * 512:(ch + 1) * 512], 8,
                                      lambda kc: [Wb, hb[s]])
                    A(S, "dve", "tensor_copy", reads=[pb], writes=[vob[tb * 2 + ch]], out=vo[:, tb, ch * 512:(ch + 1) * 512], in_=pt[:])
            store(S, dr["v"].rearrange("(j p) e -> p j e", p=128)[:, 4 * tt:4 * tt + 4, :], vo[:], vob)
            for gi in range(3):
                gs = gi % 2
                for c in range(8):
                    cc = 8 + gi * 8 + c
                    pt, pb = mm_group(S, pr, lambda kc: W[:, kc, cc * 128:(cc + 1) * 128], lambda kc: h[s][:, kc, :], 8, lambda kc: [Wb, hb[s]])
                    A(S, "act", "activation", reads=[pb], writes=[gob[gs][c]], out=go[gs][:, c, :], in_=pt[:], func=AF.Sigmoid)
                store(S, fm(dr["gatesT"], gi * 8, gi * 8 + 8, t0), go[gs][:], gob[gs])
        S.flush()
def phase_conv(S, L, dr, inp, ntiles=NT):
    HW = TT + 30
    with ExitStack() as es:
        S.es = es
        W = S.sb("Wc", [128, 4, 1024], BF16); Wb = Buf("Wc")
        ident = S.sb("ident", [128, 128], F32); idb = Buf("ident")
        dwc = S.sb("dwc", [128, 4, 31], F32); dwb = Buf("dwc")
        prm = S.sb("prm", [128, 3, 4], F32); prmb = Buf("prm")
        diag = S.sb("diag", [128, 124, 128], BF16); dgb = bufs("dg", 124)
        ones = S.sb("ones", [128, 128], BF16); onesb = Buf("ones")
        epsT = S.sb("eps", [128, 1], F32); epsb = Buf("eps")
        gt = [S.sb(f"gt{i}", [128, 4, HW], BF16) for i in range(2)]; gtb = bufs("gt", 2)
        y = S.sb("y", [128, 4, TT], F32); yb = bufs("y", 4)
        ybf = S.sb("ybf", [128, 4, TT], BF16); ybfb = bufs("ybf", 4)
        ysq = S.sb("ysq", [128, 4, TT], BF16); ysqb = bufs("ysq", 4)
        mean = S.sb("mean", [128, TT], F32); meanb = Buf("mean")
        m2 = S.sb("m2", [128, TT], F32); m2b = Buf("m2")
        var = S.sb("var", [128, TT], F32); varb = Buf("var")
        rs = S.sb("rs", [128, TT], F32); rsb = Buf("rs")
        rstd = S.sb("rstd", [128, TT], F32); rstdb = Buf("rstd")
        t1 = S.sb("t1", [128, 4, TT], F32); t1b = bufs("t1", 4)
        t2 = S.sb("t2", [128, 4, TT], F32); t2b = bufs("t2", 4)
        z = S.sb("z", [128, 4, TT], BF16); zb = bufs("z", 4)
        yo = S.sb("yo", [128, 8, TT], BF16); yob = bufs("yo", 8)
        pr = PsRot(S, 5)
        psm = S.ps("psm"); psmb = Buf("psm")
        psq = S.ps("psq"); psqb = Buf("psq")

        A(S, "dve", "memset", writes=[onesb], ap=ones[:], constant=1.0)
        A(S, "dve", "memset", writes=[epsb], ap=epsT[:], constant=1e-5)
        load(S, ident[:], inp["ident"], idb)
        load(S, dwc[:], inp["conv_dw_c"][L], dwb)
        load(S, prm[:], inp["conv_prm"][L], prmb)
        load_w_cast(S, W, Wb, inp["w_conv_out"][L], 4, 1024)
        for c in range(4):
            for k in range(31):
                A(S, "dve", "tensor_scalar", reads=[idb, dwb], writes=[dgb[c * 31 + k]], out=diag[:, c * 31 + k, :], in0=ident[:],
                  scalar1=dwc[:, c, k:k + 1], scalar2=None, op0=ALU.mult)
        A(S, "dve", "memset", writes=[gtb[0]], ap=gt[0][:, :, 0:30], constant=0.0)
        load(S, gt[0][:, :, 30:HW], fm(dr["gT"], 0, 4, 0), gtb[0])
        for tt in range(ntiles):
            s = tt % 2
            t0 = tt * TT
            if tt + 1 < ntiles:
                load(S, gt[1 - s][:], fm(dr["gT"], 0, 4, t0 + TT - 30, HW), gtb[1 - s])
            for c in range(4):
                pt, pb = pr.next()
                for k in range(31):
                    A(S, "pe", "matmul", reads=[dgb[c * 31 + k], gtb[s]], writes=[pb], out=pt[:], lhsT=diag[:, c * 31 + k, :], rhs=gt[s][:, c, k:k + TT],
                      start=(k == 0), stop=(k == 30))
                A(S, "act", "activation", reads=[pb, prmb], writes=[yb[c]], out=y[:, c, :], in_=pt[:], func=AF.Identity, bias=prm[:, 0, c:c + 1], scale=1.0)
                A(S, "dve", "tensor_copy", reads=[yb[c]], writes=[ybfb[c]], out=ybf[:, c, :], in_=y[:, c, :])
                A(S, "act", "activation", reads=[yb[c]], writes=[ysqb[c]], out=ysq[:, c, :], in_=y[:, c, :], func=AF.Square)
            for c in range(4):
                A(S, "pe", "matmul", reads=[ybfb[c], onesb], writes=[psmb], out=psm[:], lhsT=ones[:], rhs=ybf[:, c, :], start=(c == 0), stop=(c == 3))
            for c in range(4):
                A(S, "pe", "matmul", reads=[ysqb[c], onesb], writes=[psqb], out=psq[:], lhsT=ones[:], rhs=ysq[:, c, :], start=(c == 0), stop=(c == 3))
            A(S, "act", "mul", reads=[psmb], writes=[meanb], out=mean[:], in_=psm[:], mul=1.0 / 512)
            A(S, "dve", "tensor_tensor", reads=[meanb], writes=[m2b], out=m2[:], in0=mean[:], in1=mean[:], op=ALU.mult)
            A(S, "dve", "scalar_tensor_tensor", reads=[psqb, m2b], writes=[varb], out=var[:], in0=psq[:], scalar=1.0 / 512, in1=m2[:],
              op0=ALU.mult, op1=ALU.subtract)
            A(S, "act", "activation", reads=[varb, epsb], writes=[rsb], out=rs[:], in_=var[:], func=AF.Sqrt, bias=epsT[:, 0:1], scale=1.0)
            A(S, "dve", "reciprocal", reads=[rsb], writes=[rstdb], out=rstd[:], in_=rs[:])
            for c in range(4):
                A(S, "pool", "tensor_tensor", reads=[yb[c], meanb], writes=[t1b[c]], out=t1[:, c, :], in0=y[:, c, :], in1=mean[:], op=ALU.subtract)
                A(S, "dve", "tensor_tensor", reads=[t1b[c], rstdb], writes=[t2b[c]], out=t2[:, c, :], in0=t1[:, c, :], in1=rstd[:], op=ALU.mult)
                A(S, "act", "activation", reads=[t2b[c], prmb], writes=[zb[c]], out=z[:, c, :], in_=t2[:, c, :], func=AF.Silu,
                  bias=prm[:, 2, c:c + 1], scale=prm[:, 1, c:c + 1])
            for cc in range(8):
                pt, pb = mm_group(S, pr, lambda kc: W[:, kc, cc * 128:(cc + 1) * 128], lambda kc: z[:, kc, :], 4, lambda kc: [Wb, zb[kc]])
                A(S, "dve" if cc % 2 else "act", "tensor_copy" if cc % 2 else "copy", reads=[pb], writes=[yob[cc]], out=yo[:, cc, :], in_=pt[:])
            store(S, fm(dr["ybT"], 0, 8, t0), yo[:], yob)
        S.flush()


def phase_linear(S, src, dst, wsrc, kcn, ncc, ntiles=NT):
    with ExitStack() as es:
        S.es = es
        W = S.sb("Wl", [128, kcn, ncc * 128], BF16); Wb = Buf("Wl")
        it = [S.sb(f"it{i}", [128, kcn, TT], BF16) for i in range(2)]; itb = bufs("it", 2)
        ot = [S.sb(f"ot{i}", [128, ncc, TT], BF16) for i in range(2)]; otb = [bufs(f"ot{i}_", ncc) for i in range(2)]
        pr = PsRot(S, 6)
        load_w_cast(S, W, Wb, wsrc, kcn, ncc * 128, maxc=1024)
        load(S, it[0][:], fm(src, 0, kcn, 0), itb[0])
        for tt in range(ntiles):
            s = tt % 2
            t0 = tt * TT
            if tt + 1 < ntiles:
                load(S, it[1 - s][:], fm(src, 0, kcn, t0 + TT), itb[1 - s])
            for cc in range(ncc):
                pt, pb = mm_group(S, pr, lambda kc: W[:, kc, cc * 128:(cc + 1) * 128], lambda kc: it[s][:, kc, :], kcn, lambda kc: [Wb, itb[s]])
                A(S, "dve" if cc % 2 else "act", "tensor_copy" if cc % 2 else "copy", reads=[pb], writes=[otb[s][cc]], out=ot[s][:, cc, :], in_=pt[:])
            store(S, fm(dst, 0, ncc, t0), ot[s][:], otb[s])
        S.flush()


def phase_merge(S, L, xsrc, outT, dr, inp, ntiles=NT):
    with ExitStack() as es:
        S.es = es
        W = S.sb("Wo", [128, 8, 1024], BF16); Wb = Buf("Wo")
        yt = [[S.sb(f"y{j}_{i}", [128, 8, TT], BF16) for i in range(2)] for j in range(3)]
        ytb = [bufs(f"y{j}_", 2) for j in range(3)]
        gt = S.sb("gt", [128, 24, TT], BF16); gtb = bufs("gt", 3)
        xt = [S.sb(f"x{i}", [128, 8, TT], F32) for i in range(2)]; xb = bufs("x", 2)
        ta = S.sb("ta", [128, 2, TT], F32); tab = bufs("ta", 2)
        tb_ = S.sb("tb", [128, 2, TT], F32); tbb = bufs("tb", 2)
        tc_ = S.sb("tc", [128, 2, TT], F32); tcb = bufs("tc", 2)
        m = S.sb("m", [128, 8, TT], BF16); mb = bufs("m", 8)
        osb = S.sb("osb", [128, 8, TT], F32); osbb = bufs("osb", 8)
        sq = S.sb("sq", [128, 8, TT], BF16); sqb = bufs("sq", 8)
        h2 = S.sb("h2", [128, 8, TT], BF16); h2b = bufs("h2", 8)
        rs = S.sb("rs", [128, TT], F32); rsb = Buf("rs")
        rstd = S.sb("rstd", [128, TT], F32); rstdb = Buf("rstd")
        gcol = S.sb("gcol", [128, 2, 8], F32); gcb = Buf("gcol")
        ones = S.sb("ones", [128, 128], BF16); onesb = Buf("ones")
        epsT = S.sb("eps", [128, 1], F32); epsb = Buf("eps")
        pr = PsRot(S, 6)
        pss = S.ps("pss"); pssb = Buf("pss")
        A(S, "dve", "memset", writes=[onesb], ap=ones[:], constant=1.0)
        A(S, "dve", "memset", writes=[epsb], ap=epsT[:], constant=1e-6)
        load(S, gcol[:, 0, :], inp["g_post_mix"][L], gcb)
        load(S, gcol[:, 1, :], inp["g_pre_ffn"][L], gcb)
        load_w_cast(S, W, Wb, inp["w_out"][L], 8, 1024, maxc=1024)
        ysrc = [dr["yaT"], dr["ybT"], dr["ycT"]]

        def loads(tt):
            s = tt % 2
            for j in range(3):
                load(S, yt[j][s][:], fm(ysrc[j], 0, 8, tt * TT), ytb[j][s])
            load(S, xt[s][:], fm(xsrc, 0, 8, tt * TT), xb[s])

        loads(0)
        for tt in range(ntiles):
            s = tt % 2
            t0 = tt * TT
            for j in range(3):
                load(S, gt[:, j * 8:(j + 1) * 8, :], fm(dr["gatesT"], j * 8, j * 8 + 8, t0), gtb[j])
            if tt + 1 < ntiles:
                loads(tt + 1)
            for kc in range(8):
                r = kc % 2
                A(S, "dve", "tensor_tensor", reads=[gtb[0], ytb[0][s]], writes=[tab[r]], out=ta[:, r, :], in0=gt[:, kc, :], in1=yt[0][s][:, kc, :], op=ALU.mult)
                A(S, "pool", "tensor_tensor", reads=[gtb[1], ytb[1][s]], writes=[tbb[r]], out=tb_[:, r, :], in0=gt[:, 8 + kc, :], in1=yt[1][s][:, kc, :], op=ALU.mult)
                A(S, "pool", "tensor_tensor", reads=[gtb[2], ytb[2][s]], writes=[tcb[r]], out=tc_[:, r, :], in0=gt[:, 16 + kc, :], in1=yt[2][s][:, kc, :], op=ALU.mult)
                A(S, "dve", "tensor_tensor", reads=[tab[r], tbb[r]], writes=[tab[r]], out=ta[:, r, :], in0=ta[:, r, :], in1=tb_[:, r, :], op=ALU.add)
                A(S, "dve", "tensor_tensor", reads=[tab[r], tcb[r]], writes=[mb[kc]], out=m[:, kc, :], in0=ta[:, r, :], in1=tc_[:, r, :], op=ALU.add)
            for cc in range(8):
                pt, pb = mm_group(S, pr, lambda kc: W[:, kc, cc * 128:(cc + 1) * 128], lambda kc: m[:, kc, :], 8, lambda kc: [Wb, mb[kc]])
                A(S, "act", "copy", reads=[pb], writes=[osbb[cc]], out=osb[:, cc, :], in_=pt[:])
                A(S, "act", "activation", reads=[pb], writes=[sqb[cc]], out=sq[:, cc, :], in_=pt[:], func=AF.Square)
            rms_rstd(S, [sq[:, kc, :] for kc in range(8)], sqb, ones[:], pss[:], pssb, rs[:], rsb, rstd[:], rstdb, epsT[:, 0:1], 1.0 / D)
            for kc in range(8):
                A(S, "dve", "scalar_tensor_tensor", reads=[osbb[kc], gcb, rstdb], writes=[osbb[kc]], out=osb[:, kc, :], in0=osb[:, kc, :],
                  scalar=gcol[:, 0, kc:kc + 1], in1=rstd[:], op0=ALU.mult, op1=ALU.mult)
                A(S, "pool", "tensor_tensor", reads=[osbb[kc], xb[s]], writes=[osbb[kc]], out=osb[:, kc, :], in0=osb[:, kc, :], in1=xt[s][:, kc, :], op=ALU.add)
                A(S, "act", "activation", reads=[osbb[kc]], writes=[sqb[kc]], out=sq[:, kc, :], in_=osb[:, kc, :], func=AF.Square)
            store(S, fm(outT, 0, 8, t0), osb[:], osbb)
            rms_rstd(S, [sq[:, kc, :] for kc in range(8)], sqb, ones[:], pss[:], pssb, rs[:], rsb, rstd[:], rstdb, epsT[:, 0:1], 1.0 / D)
            for kc in range(8):
                A(S, "dve", "scalar_tensor_tensor", reads=[osbb[kc], gcb, rstdb], writes=[h2b[kc]], out=h2[:, kc, :], in0=osb[:, kc, :],
                  scalar=gcol[:, 1, kc:kc + 1], in1=rstd[:], op0=ALU.mult, op1=ALU.mult)
            store(S, fm(dr["h2T"], 0, 8, t0), h2[:], h2b)
        S.flush()


def phase_ffn_in(S, L, dr, inp, ntiles=NT):
    with ExitStack() as es:
        S.es = es
        W = S.sb("Wf", [128, 8, 2 * DFF], BF16); Wb = Buf("Wf")
        it = [S.sb(f"it{i}", [128, 8, TT], BF16) for i in range(2)]; itb = bufs("it", 2)
        fo = S.sb("fo", [128, NFC, TT], BF16); fob = bufs("fo", NFC)
        sl = S.sb("sl", [128, 2, TT], BF16); slb = bufs("sl", 2)
        pr = PsRot(S, 7)
        load_w_cast(S, W, Wb, inp["w_ffn_in"][L], 8, 2 * DFF, maxc=1408)
        load(S, it[0][:], fm(dr["h2T"], 0, 8, 0), itb[0])
        for tt in range(ntiles):
            s = tt % 2
            t0 = tt * TT
            if tt + 1 < ntiles:
                load(S, it[1 - s][:], fm(dr["h2T"], 0, 8, t0 + TT), itb[1 - s])
            for j in range(NFC):
                r = j % 2
                pt, pb = mm_group(S, pr, lambda kc: W[:, kc, j * 128:(j + 1) * 128], lambda kc: it[s][:, kc, :], 8, lambda kc: [Wb, itb[s]])
                A(S, "act", "activation", reads=[pb], writes=[slb[r]], out=sl[:, r, :], in_=pt[:], func=AF.Silu)
                pt2, pb2 = mm_group(S, pr, lambda kc: W[:, kc, DFF + j * 128:DFF + (j + 1) * 128], lambda kc: it[s][:, kc, :], 8, lambda kc: [Wb, itb[s]])
                A(S, "dve", "tensor_tensor", reads=[pb2, slb[r]], writes=[fob[j]], out=fo[:, j, :], in0=pt2[:], in1=sl[:, r, :], op=ALU.mult)
            store(S, fm(dr["fT"], 0, 11, t0), fo[:, 0:11, :], fob[0:11])
            store(S, fm(dr["fT"], 11, 22, t0), fo[:, 11:22, :], fob[11:22])
        S.flush()


def phase_ffn_out(S, L, outT, dr, inp, ntiles=NT):
    with ExitStack() as es:
        S.es = es
        W = S.sb("Wfo", [128, NFC, 1024], BF16); Wb = Buf("Wfo")
        it = [S.sb(f"it{i}", [128, NFC, TT], BF16) for i in range(2)]; itb = bufs("it", 2)
        xt = [S.sb(f"x{i}", [128, 8, TT], F32) for i in range(2)]; xb = bufs("x", 2)
        osb = S.sb("osb", [128, 8, TT], F32); osbb = bufs("osb", 8)
        sq = S.sb("sq", [128, 8, TT], BF16); sqb = bufs("sq", 8)
        rs = S.sb("rs", [128, TT], F32); rsb = Buf("rs")
        rstd = S.sb("rstd", [128, TT], F32); rstdb = Buf("rstd")
        gcol = S.sb("gcol", [128, 8], F32); gcb = Buf("gcol")
        ones = S.sb("ones", [128, 128], BF16); onesb = Buf("ones")
        epsT = S.sb("eps", [128, 1], F32); epsb = Buf("eps")
        pr = PsRot(S, 6)
        pss = S.ps("pss"); pssb = Buf("pss")
        A(S, "dve", "memset", writes=[onesb], ap=ones[:], constant=1.0)
        A(S, "dve", "memset", writes=[epsb], ap=epsT[:], constant=1e-6)
        load(S, gcol[:], inp["g_post_ffn"][L], gcb)
        load_w_cast(S, W, Wb, inp["w_ffn_out"][L], NFC, 1024, maxc=1024)

        def loads(tt):
            s = tt % 2
            load(S, it[s][:], fm(dr["fT"], 0, NFC, tt * TT), itb[s])
            load(S, xt[s][:], fm(outT, 0, 8, tt * TT), xb[s])

        loads(0)
        for tt in range(ntiles):
            s = tt % 2
            t0 = tt * TT
            if tt + 1 < ntiles:
                loads(tt + 1)
            for cc in range(8):
                pt, pb = mm_group(S, pr, lambda kc: W[:, kc, cc * 128:(cc + 1) * 128], lambda kc: it[s][:, kc, :], NFC, lambda kc: [Wb, itb[s]])
                A(S, "act", "copy", reads=[pb], writes=[osbb[cc]], out=osb[:, cc, :], in_=pt[:])
                A(S, "act", "activation", reads=[pb], writes=[sqb[cc]], out=sq[:, cc, :], in_=pt[:], func=AF.Square)
            rms_rstd(S, [sq[:, kc, :] for kc in range(8)], sqb, ones[:], pss[:], pssb, rs[:], rsb, rstd[:], rstdb, epsT[:, 0:1], 1.0 / D)
            for kc in range(8):
                A(S, "dve", "scalar_tensor_tensor", reads=[osbb[kc], gcb, rstdb], writes=[osbb[kc]], out=osb[:, kc, :], in0=osb[:, kc, :],
                  scalar=gcol[:, kc:kc + 1], in1=rstd[:], op0=ALU.mult, op1=ALU.mult)
                A(S, "pool", "tensor_tensor", reads=[osbb[kc], xb[s]], writes=[osbb[kc]], out=osb[:, kc, :], in0=osb[:, kc, :], in1=xt[s][:, kc, :], op=ALU.add)
            store(S, fm(outT, 0, 8, t0), osb[:], osbb)
        S.flush()
TWO_PI = 2.0 * math.pi


def sincos(S, x, xb, tmp, tmpb, out_s, out_sb, out_c, out_cb, pib):
    MAGIC = 12582912.0
    A(S, "dve", "tensor_scalar", reads=[xb], writes=[tmpb], out=tmp, in0=x, scalar1=MAGIC, scalar2=None, op0=ALU.add)
    A(S, "dve", "tensor_scalar", reads=[tmpb], writes=[tmpb], out=tmp, in0=tmp, scalar1=-MAGIC, scalar2=None, op0=ALU.add)
    A(S, "dve", "tensor_tensor", reads=[xb, tmpb], writes=[tmpb], out=tmp, in0=x, in1=tmp, op=ALU.subtract)
    A(S, "act", "activation", reads=[tmpb], writes=[out_sb], out=out_s, in_=tmp, func=AF.Sin, scale=TWO_PI)
    A(S, "act", "activation", reads=[tmpb], writes=[out_cb], out=out_c, in_=tmp, func=AF.Sin, scale=math.pi)
    A(S, "dve", "tensor_tensor", reads=[out_cb], writes=[out_cb], out=out_c, in0=out_c, in1=out_c, op=ALU.mult)
    A(S, "dve", "tensor_scalar", reads=[out_cb], writes=[out_cb], out=out_c, in0=out_c, scalar1=-2.0, scalar2=1.0, op0=ALU.mult, op1=ALU.add)


def phase_ssm_setup(S, L, dr, inp):
    N = 2048
    with ExitStack() as es:
        S.es = es
        rowp = S.sb("rowp", [128, 3, N], F32); rowb = Buf("rowp")
        colp = S.sb("colp", [128, 3, 16], F32); colb = Buf("colp")
        Bst = S.sb("Bst", [128, 2, N], F32); Bstb = Buf("Bst")
        iota = S.sb("iota", [128, 513], F32); iotab = Buf("iota")
        piT = S.sb("piT", [128, 1], F32); pitb = Buf("piT")
        names = ["dt", "ar", "th", "r", "f", "fc", "sn", "cs", "lbr", "lbi", "den", "kr", "ki", "ta", "tb"]
        tm = {n: S.sb("s_" + n, [128, N], F32) for n in names}
        tb = {n: Buf("s_" + n) for n in names}
        wb = S.sb("wbo", [128, 2, N], BF16); wbb = bufs("wbo", 2)
        cdt = S.sb("cdt", [128, 16], F32); cdtb = Buf("cdt")
        car = S.sb("car", [128, 16], F32); carb = Buf("car")
        cth = S.sb("cth", [128, 16], F32); cthb = Buf("cth")
        rcol = S.sb("rcol", [128, 16], F32); rcolb = Buf("rcol")
        ff = S.sb("ff", [128, 2, 513], F32); ffb = bufs("ff", 2)
        ft = S.sb("ft", [128, 2, 513], F32); ftb = bufs("ft", 2)
        tabs = S.sb("tabs", [128, 2, 2, 513], F32); tabsb = [bufs(f"tabs{i}_", 2) for i in range(2)]

        A(S, "dve", "memset", writes=[pitb], ap=piT[:], constant=math.pi)
        load(S, rowp[:], inp["ssm_rows"][L], rowb)
        load(S, colp[:], inp["ssm_cols"][L], colb)
        load(S, Bst[:], inp["ssm_bblk"][L], Bstb)
        load(S, iota[:], inp["iota"], iotab)
        are, aim, ldt = rowp[:, 0, :], rowp[:, 1, :], rowp[:, 2, :]

        def T_(eng, _opn, o, ins, **kw):
            A(S, eng, _opn, reads=[rowb] + [tb[i] for i in ins if i in tb], writes=[tb[o]], **kw)

        T_("act", "activation", "dt", [], out=tm["dt"][:], in_=ldt, func=AF.Exp)
        T_("dve", "tensor_tensor", "ar", ["dt"], out=tm["ar"][:], in0=are, in1=tm["dt"][:], op=ALU.mult)
        T_("dve", "tensor_tensor", "th", ["dt"], out=tm["th"][:], in0=aim, in1=tm["dt"][:], op=ALU.mult)
        T_("act", "activation", "r", ["ar"], out=tm["r"][:], in_=tm["ar"][:], func=AF.Exp)
        T_("dve", "tensor_scalar", "f", ["th"], out=tm["f"][:], in0=tm["th"][:], scalar1=1.0 / TWO_PI, scalar2=None, op0=ALU.mult)
        sincos(S, tm["f"][:], tb["f"], tm["fc"][:], tb["fc"], tm["sn"][:], tb["sn"], tm["cs"][:], tb["cs"], (piT[:, 0:1], pitb))
        T_("dve", "tensor_tensor", "lbr", ["r", "cs"], out=tm["lbr"][:], in0=tm["r"][:], in1=tm["cs"][:], op=ALU.mult)
        T_("dve", "tensor_tensor", "lbi", ["r", "sn"], out=tm["lbi"][:], in0=tm["r"][:], in1=tm["sn"][:], op=ALU.mult)
        T_("dve", "tensor_scalar", "lbr", ["lbr"], out=tm["lbr"][:], in0=tm["lbr"][:], scalar1=-1.0, scalar2=None, op0=ALU.add)
        T_("dve", "tensor_tensor", "ta", [], out=tm["ta"][:], in0=are, in1=are, op=ALU.mult)
        T_("dve", "tensor_tensor", "tb", [], out=tm["tb"][:], in0=aim, in1=aim, op=ALU.mult)
        T_("dve", "tensor_tensor", "den", ["ta", "tb"], out=tm["den"][:], in0=tm["ta"][:], in1=tm["tb"][:], op=ALU.add)
        T_("dve", "reciprocal", "den", ["den"], out=tm["den"][:], in_=tm["den"][:])
        T_("dve", "tensor_tensor", "ta", ["lbr"], out=tm["ta"][:], in0=tm["lbr"][:], in1=are, op=ALU.mult)
        T_("dve", "tensor_tensor", "tb", ["lbi"], out=tm["tb"][:], in0=tm["lbi"][:], in1=aim, op=ALU.mult)
        T_("dve", "tensor_tensor", "kr", ["ta", "tb"], out=tm["kr"][:], in0=tm["ta"][:], in1=tm["tb"][:], op=ALU.add)
        T_("dve", "tensor_tensor", "kr", ["kr", "den"], out=tm["kr"][:], in0=tm["kr"][:], in1=tm["den"][:], op=ALU.mult)
        T_("dve", "tensor_tensor", "ta", ["lbi"], out=tm["ta"][:], in0=tm["lbi"][:], in1=are, op=ALU.mult)
        T_("dve", "tensor_tensor", "tb", ["lbr"], out=tm["tb"][:], in0=tm["lbr"][:], in1=aim, op=ALU.mult)
        T_("dve", "tensor_tensor", "ki", ["ta", "tb"], out=tm["ki"][:], in0=tm["ta"][:], in1=tm["tb"][:], op=ALU.subtract)
        T_("dve", "tensor_tensor", "ki", ["ki", "den"], out=tm["ki"][:], in0=tm["ki"][:], in1=tm["den"][:], op=ALU.mult)
        Br, Bi = Bst[:, 0, :], Bst[:, 1, :]
        A(S, "dve", "tensor_tensor", reads=[tb["kr"], Bstb], writes=[tb["ta"]], out=tm["ta"][:], in0=tm["kr"][:], in1=Br, op=ALU.mult)
        A(S, "dve", "tensor_tensor", reads=[tb["ki"], Bstb], writes=[tb["tb"]], out=tm["tb"][:], in0=tm["ki"][:], in1=Bi, op=ALU.mult)
        A(S, "dve", "tensor_tensor", reads=[tb["ta"], tb["tb"]], writes=[wbb[0]], out=wb[:, 0, :], in0=tm["ta"][:], in1=tm["tb"][:], op=ALU.subtract)
        A(S, "dve", "tensor_tensor", reads=[tb["kr"], Bstb], writes=[tb["ta"]], out=tm["ta"][:], in0=tm["kr"][:], in1=Bi, op=ALU.mult)
        A(S, "dve", "tensor_tensor", reads=[tb["ki"], Bstb], writes=[tb["tb"]], out=tm["tb"][:], in0=tm["ki"][:], in1=Br, op=ALU.mult)
        A(S, "dve", "tensor_tensor", reads=[tb["ta"], tb["tb"]], writes=[wbb[1]], out=wb[:, 1, :], in0=tm["ta"][:], in1=tm["tb"][:], op=ALU.add)
        store(S, dr["ssm_wb"], wb[:], wbb)
        A(S, "act", "activation", reads=[colb], writes=[cdtb], out=cdt[:], in_=colp[:, 2, :], func=AF.Exp)
        A(S, "dve", "tensor_tensor", reads=[colb, cdtb], writes=[carb], out=car[:], in0=colp[:, 0, :], in1=cdt[:], op=ALU.mult)
        A(S, "act", "activation", reads=[carb], writes=[rcolb], out=rcol[:], in_=car[:], func=AF.Exp)
        store(S, dr["ssm_rcol"], rcol[:], rcolb)
        A(S, "dve", "tensor_tensor", reads=[colb, cdtb], writes=[cthb], out=cth[:], in0=colp[:, 1, :], in1=cdt[:], op=ALU.mult)
        A(S, "dve", "tensor_scalar", reads=[cthb], writes=[cthb], out=cth[:], in0=cth[:], scalar1=1.0 / TWO_PI, scalar2=None, op0=ALU.mult)
        for i in range(16):
            s = i % 2
            A(S, "dve", "tensor_scalar", reads=[iotab, cthb], writes=[ffb[s]], out=ff[:, s, :], in0=iota[:], scalar1=cth[:, i:i + 1], scalar2=None,
              op0=ALU.mult)
            sincos(S, ff[:, s, :], ffb[s], ft[:, s, :], ftb[s], tabs[:, s, 1, :], tabsb[s][1], tabs[:, s, 0, :], tabsb[s][0], (piT[:, 0:1], pitb))
            store(S, dr["ssm_tab"][:, i, :, :], tabs[:, s, :, :], tabsb[s])
        S.flush()


def phase_ssm(S, L, dr, inp, ntiles=NT):
    with ExitStack() as es:
        S.es = es
        tab = S.sb("tab", [128, 16, 2, 513], F32); tabb = Buf("tab")
        WB = S.sb("WB", [128, 2, 2048], BF16); WBb = Buf("WB")
        WC = S.sb("WC", [128, 2, 2048], BF16); WCb = Buf("WC")
        WCn = S.sb("WCn", [128, 2048], BF16); WCnb = Buf("WCn")
        Wg = S.sb("Wg", [128, 4, 1024], BF16); Wgb = Buf("Wg")
        Wo = S.sb("Wo", [128, 4, 1024], BF16); Wob = Buf("Wo")
        rcol = S.sb("rcol", [128, 16], F32); rcolb = Buf("rcol")
        dcol = S.sb("dcol", [128, 4], F32); dcolb = Buf("dcol")
        rot = S.sb("rot", [128, 16, 3], F32); rotb = Buf("rot")
        car = S.sb("car", [128, 16, 2], F32); carb = bufs("car", 16)
        ab = S.sb("ab", [128, 2, 2], F32); abb = bufs("ab", 2)
        ut = [S.sb(f"u{i}", [128, 4, TT], BF16) for i in range(2)]; utb = bufs("u", 2)
        NW = 2
        wk = {n: S.sb("w_" + n, [128, NW, TT], F32) for n in ["t1", "t2", "t3", "t4", "mr", "mi", "wr", "wi", "t5", "t6", "t7", "t8"]}
        wkb = {n: bufs("w_" + n, NW) for n in wk}
        xr = S.sb("xr", [128, NW, TT], BF16); xrb = bufs("xr", NW)
        xi = S.sb("xi", [128, NW, TT], BF16); xib = bufs("xi", NW)
        ysk = S.sb("ysk", [128, 2, TT], F32); yskb = bufs("ysk", 2)
        x2 = S.sb("x2", [128, 2, TT], F32); x2b = bufs("x2", 2)
        yg = S.sb("yg", [128, 4, TT], BF16); ygb = bufs("yg", 4)
        sl = S.sb("sl", [128, 2, TT], BF16); slb = bufs("sl", 2)
        sgl = S.sb("sgl", [128, 4, TT], BF16); sglb = bufs("sgl", 4)
        yo = S.sb("yo", [128, 8, TT], BF16); yob = bufs("yo", 8)
        pv = PsRot(S, 4, "pv")
        py = PsRot(S, 2, "py")
        pg = PsRot(S, 2, "pg")

        load(S, tab[:], dr["ssm_tab"], tabb)
        load(S, WB[:], dr["ssm_wb"], WBb)
        load(S, rcol[:], dr["ssm_rcol"], rcolb)
        load(S, dcol[:], inp["ssm_dcol"][L], dcolb)
        for j in range(2):
            A(S, "pool", "dma_start", writes=[WCb], dma=True, out=WC[:, j, :], in_=inp["ssm_cblk"][L][:, j, :])
        load_w_cast(S, Wg, Wgb, inp["w_ssm_glu"][L], 4, 1024, maxc=1024)
        load_w_cast(S, Wo, Wob, inp["w_ssm_out"][L], 4, 1024, maxc=1024)
        A(S, "act", "mul", reads=[WCb], writes=[WCnb], out=WCn[:], in_=WC[:, 1, :], mul=-1.0)
        A(S, "dve", "tensor_copy", reads=[tabb], writes=[rotb], out=rot[:, :, 0:2], in_=tab[:, :, :, 512])
        A(S, "dve", "tensor_scalar", reads=[tabb, rotb], writes=[rotb], out=rot[:, :, 2], in0=tab[:, :, 1, 512], scalar1=-1.0, scalar2=None, op0=ALU.mult)
        for i in range(16):
            A(S, "dve", "memset", writes=[carb[i]], ap=car[:, i, :], constant=0.0)
        load(S, ut[0][:], fm(dr["uT"], 0, 4, 0), utb[0])
        for tt in range(ntiles):
            s = tt % 2
            t0 = tt * TT
            if tt + 1 < ntiles:
                load(S, ut[1 - s][:], fm(dr["uT"], 0, 4, t0 + TT), utb[1 - s])
            ypd = {}

            def tile_gen(i):
                q = i // 4
                w = i % NW
                C_ = tab[:, i, 0, 0:TT]
                S_ = tab[:, i, 1, 0:TT]
                vr, vrb = pv.next()
                vi, vib = pv.next()
                A(S, "pe", "matmul", reads=[WBb, utb[s]], writes=[vrb], out=vr[:], lhsT=WB[:, 0, i * 128:(i + 1) * 128], rhs=ut[s][:, q, :], start=True, stop=True)
                yield
                A(S, "pe", "matmul", reads=[WBb, utb[s]], writes=[vib], out=vi[:], lhsT=WB[:, 1, i * 128:(i + 1) * 128], rhs=ut[s][:, q, :], start=True, stop=True)
                yield

                def W_(n):
                    return wk[n][:, w, :]

                def dv(eng, o, in0, in1, op, rd):
                    A(S, eng, "tensor_tensor", reads=rd, writes=[wkb[o][w]], out=W_(o), in0=in0, in1=in1, op=op)

                dv("dve", "t1", vr[:], C_, ALU.mult, [vrb, tabb])
                yield
                dv("dve", "t2", vi[:], S_, ALU.mult, [vib, tabb])
                yield
                dv("dve", "mr", W_("t1"), W_("t2"), ALU.add, [wkb["t1"][w], wkb["t2"][w]])
                yield
                dv("dve", "t3", vi[:], C_, ALU.mult, [vib, tabb])
                yield
                dv("dve", "t4", vr[:], S_, ALU.mult, [vrb, tabb])
                yield
                dv("dve", "mi", W_("t3"), W_("t4"), ALU.subtract, [wkb["t3"][w], wkb["t4"][w]])
                yield
                rb = rcol[:, i:i + 1].to_broadcast([128, TT])
                A(S, "dve", "tensor_tensor_scan", reads=[wkb["mr"][w], rcolb, carb[i]], writes=[wkb["wr"][w]], out=W_("wr"), data0=rb, data1=W_("mr"),
                  initial=car[:, i, 0:1], op0=ALU.mult, op1=ALU.add)
                yield
                A(S, "dve", "tensor_tensor_scan", reads=[wkb["mi"][w], rcolb, carb[i]], writes=[wkb["wi"][w]], out=W_("wi"), data0=rb, data1=W_("mi"),
                  initial=car[:, i, 1:2], op0=ALU.mult, op1=ALU.add)
                yield
                wrl = wk["wr"][:, w, TT - 1:TT]
                wil = wk["wi"][:, w, TT - 1:TT]
                k = i % 2
                A(S, "dve", "tensor_tensor", reads=[wkb["wr"][w], rotb], writes=[abb[k]], out=ab[:, k, 0:1], in0=wrl, in1=rot[:, i, 0:1], op=ALU.mult)
                yield
                A(S, "dve", "tensor_tensor", reads=[wkb["wr"][w], rotb, abb[k]], writes=[abb[k]], out=ab[:, k, 1:2], in0=wrl, in1=rot[:, i, 1:2], op=ALU.mult)
                yield
                A(S, "dve", "scalar_tensor_tensor", reads=[wkb["wi"][w], rotb, abb[k]], writes=[carb[i]], out=car[:, i, 0:1], in0=wil, scalar=rot[:, i, 2:3],
                  in1=ab[:, k, 0:1], op0=ALU.mult, op1=ALU.add)
                yield
                A(S, "dve", "scalar_tensor_tensor", reads=[wkb["wi"][w], rotb, abb[k], carb[i]], writes=[carb[i]], out=car[:, i, 1:2], in0=wil, scalar=rot[:, i, 0:1],
                  in1=ab[:, k, 1:2], op0=ALU.mult, op1=ALU.add)
                yield
                dv("pool", "t5", W_("wr"), C_, ALU.mult, [wkb["wr"][w], tabb])
                yield
                dv("pool", "t6", W_("wi"), S_, ALU.mult, [wkb["wi"][w], tabb])
                yield
                A(S, "pool", "tensor_tensor", reads=[wkb["t5"][w], wkb["t6"][w]], writes=[xrb[w]], out=xr[:, w, :], in0=W_("t5"), in1=W_("t6"), op=ALU.subtract)
                yield
                dv("pool", "t7", W_("wr"), S_, ALU.mult, [wkb["wr"][w], tabb])
                yield
                dv("pool", "t8", W_("wi"), C_, ALU.mult, [wkb["wi"][w], tabb])
                yield
                A(S, "pool", "tensor_tensor", reads=[wkb["t7"][w], wkb["t8"][w]], writes=[xib[w]], out=xi[:, w, :], in0=W_("t7"), in1=W_("t8"), op=ALU.add)
                yield
                if i % 4 == 0:
                    ypd[q] = py.next()
                ypt, ypb = ypd[q]
                A(S, "pe", "matmul", reads=[WCb, xrb[w]], writes=[ypb], out=ypt[:], lhsT=WC[:, 0, i * 128:(i + 1) * 128], rhs=xr[:, w, :], start=(i % 4 == 0), stop=False)
                yield
                A(S, "pe", "matmul", reads=[WCnb, xib[w]], writes=[ypb], out=ypt[:], lhsT=WCn[:, i * 128:(i + 1) * 128], rhs=xi[:, w, :], start=False, stop=(i % 4 == 3))
                yield
                if i % 4 == 3:
                    e = q % 2
                    A(S, "dve", "scalar_tensor_tensor", reads=[utb[s], dcolb, ypb], writes=[yskb[e]], out=ysk[:, e, :], in0=ut[s][:, q, :], scalar=dcol[:, q:q + 1],
                      in1=ypt[:], op0=ALU.mult, op1=ALU.add)
                    yield
                    A(S, "act", "activation", reads=[yskb[e]], writes=[x2b[e]], out=x2[:, e, :], in_=ysk[:, e, :], func=AF.Square)
                    yield
                    A(S, "dve", "tensor_scalar", reads=[x2b[e]], writes=[x2b[e]], out=x2[:, e, :], in0=x2[:, e, :], scalar1=0.044715, scalar2=1.0, op0=ALU.mult, op1=ALU.add)
                    yield
                    A(S, "dve", "tensor_tensor", reads=[x2b[e], yskb[e]], writes=[x2b[e]], out=x2[:, e, :], in0=x2[:, e, :], in1=ysk[:, e, :], op=ALU.mult)
                    yield
                    A(S, "act", "activation", reads=[x2b[e]], writes=[x2b[e]], out=x2[:, e, :], in_=x2[:, e, :], func=AF.Sigmoid, scale=1.5957691216057308)
                    yield
                    A(S, "dve", "tensor_tensor", reads=[x2b[e], yskb[e]], writes=[ygb[q]], out=yg[:, q, :], in0=x2[:, e, :], in1=ysk[:, e, :], op=ALU.mult)
                    yield

            gens = []
            for i0 in range(0, 16, 2):
                gens = [tile_gen(i0), tile_gen(i0 + 1)]
                while gens:
                    for g_ in list(gens):
                        try:
                            next(g_)
                        except StopIteration:
                            gens.remove(g_)
            for c in range(4):
                r = c % 2
                pt, pb = mm_group(S, pg, lambda kc: Wg[:, kc, 512 + c * 128:512 + (c + 1) * 128], lambda kc: yg[:, kc, :], 4, lambda kc: [Wgb, ygb[kc]])
                A(S, "act", "activation", reads=[pb], writes=[slb[r]], out=sl[:, r, :], in_=pt[:], func=AF.Sigmoid)
                pt2, pb2 = mm_group(S, pg, lambda kc: Wg[:, kc, c * 128:(c + 1) * 128], lambda kc: yg[:, kc, :], 4, lambda kc: [Wgb, ygb[kc]])
                A(S, "dve", "tensor_tensor", reads=[pb2, slb[r]], writes=[sglb[c]], out=sgl[:, c, :], in0=pt2[:], in1=sl[:, r, :], op=ALU.mult)
            for cc in range(8):
                pt, pb = mm_group(S, pg, lambda kc: Wo[:, kc, cc * 128:(cc + 1) * 128], lambda kc: sgl[:, kc, :], 4, lambda kc: [Wob, sglb[kc]])
                A(S, "act", "copy", reads=[pb], writes=[yob[cc]], out=yo[:, cc, :], in_=pt[:])
            store(S, fm(dr["yaT"], 0, 8, t0), yo[:], yob)
        S.flush()
def phase_attn(S, L, dr, inp, lam_init, nheads=8, nqt=NT):
    VW = 130
    with ExitStack() as es:
        S.es = es
        KT = [S.sb(f"KT{i}", [128, T], BF16) for i in range(2)]; KTb = bufs("KT", 2)
        VP = [S.sb(f"VP{i}", [128, 64, VW], BF16) for i in range(2)]; VPb = bufs("VP", 2)
        QT = [S.sb(f"QT{i}", [128, TT], BF16) for i in range(2)]; QTb = bufs("QT", 2)
        PT = [S.sb(f"PT{i}", [128, 2, TT], BF16) for i in range(2)]; PTb = bufs("PT", 2)
        btab = S.sb("btab", [128, 16, 2, 128], F32); btabb = Buf("btab")
        bt = S.sb("bt", [128, 16, 2, 128], BF16); btb = Buf("bt")
        mask = S.sb("mask", [128, 128], F32); maskb = Buf("mask")
        c31 = S.sb("c31", [128, 16], F32); c31b = Buf("c31")
        identf = S.sb("identf", [128, 128], F32); identfb = Buf("identf")
        identb = S.sb("identb", [128, 128], BF16); identbb = Buf("identb")
        lamv = S.sb("lamv", [128, 4, 64], F32); lamvb = Buf("lamv")
        lp = S.sb("lp", [128, 2, 64], F32); lpb = Buf("lp")
        ls = S.sb("ls", [128, 4], F32); lsb = Buf("ls")
        gsub = S.sb("gsub", [128, 128], F32); gsubb = Buf("gsub")
        epsT = S.sb("eps", [128, 1], F32); epsb = Buf("eps")
        osb = S.sb("osb", [128, 2, 2, VW], F32); osbb = [bufs(f"osb{i}_", 2) for i in range(2)]
        rr = S.sb("rr", [128, 2, 4], F32); rrb = bufs("rr", 2)
        t2 = S.sb("t2", [128, 2, 128], F32); t2b = bufs("t2", 2)
        o = S.sb("o", [128, 2, 128], F32); ob = bufs("o", 2)
        junk = S.sb("junk", [128, 2, 128], F32); junkb = bufs("junk", 2)
        on = S.sb("on", [128, 2, 128], BF16); onb = bufs("on", 2)
        ao = [S.sb(f"ao{i}", [128, TT], BF16) for i in range(2)]; aob = [bufs(f"ao{i}_", 4) for i in range(2)]
        pS = [S.ps(f"pS{i}", (128, 2, TT)) for i in range(2)]; pSb = bufs("pS", 2)
        pA = [S.ps(f"pA{i}") for i in range(3)]; pAb = bufs("pA", 8)
        pT = S.ps("pT", (128, 128), BF16); pTb = Buf("pT")

        def acc(m, qs):
            a = m * 4 + qs
            return pA[a // 3][:, (a % 3) * 129:(a % 3) * 129 + 129], pAb[a]

        A(S, "dve", "memset", writes=[epsb], ap=epsT[:], constant=1e-5)
        load(S, btab[:], inp["att_btab"], btabb)
        load(S, mask[:], inp["att_mask"], maskb)
        load(S, c31[:], inp["att_c31"], c31b)
        load(S, identf[:], inp["ident"], identfb)
        load(S, lamv[:], inp["att_lamv"][L], lamvb)
        load(S, gsub[:], inp["att_gsub"][L], gsubb)
        A(S, "dve", "tensor_copy", reads=[identfb], writes=[identbb], out=identb[:], in_=identf[:])
        A(S, "act", "mul", reads=[gsubb], writes=[gsubb], out=gsub[:], in_=gsub[:], mul=(1.0 - lam_init))
        A(S, "dve", "tensor_tensor", reads=[lamvb], writes=[lpb], out=lp[:, 0, :], in0=lamv[:, 0, :], in1=lamv[:, 1, :], op=ALU.mult)
        A(S, "dve", "tensor_tensor", reads=[lamvb, lpb], writes=[lpb], out=lp[:, 1, :], in0=lamv[:, 2, :], in1=lamv[:, 3, :], op=ALU.mult)
        A(S, "dve", "tensor_reduce", reads=[lpb], writes=[lsb], out=ls[:, 0:2], in_=lp[:], axis=mybir.AxisListType.X, op=ALU.add)
        A(S, "act", "activation", reads=[lsb], writes=[lsb], out=ls[:, 0:2], in_=ls[:, 0:2], func=AF.Exp)
        A(S, "dve", "tensor_tensor", reads=[lsb], writes=[lsb], out=ls[:, 2:3], in0=ls[:, 0:1], in1=ls[:, 1:2], op=ALU.subtract)
        A(S, "dve", "tensor_scalar", reads=[lsb], writes=[lsb], out=ls[:, 3:4], in0=ls[:, 2:3], scalar1=float(lam_init), scalar2=None, op0=ALU.add)
        lamc = ls[:, 3:4]
        for mm in range(16):
            A(S, "dve", "tensor_scalar", reads=[btabb, c31b], writes=[btabb], out=btab[:, mm, :, :], in0=btab[:, mm, :, :], scalar1=c31[:, mm:mm + 1], scalar2=None,
              op0=ALU.subtract)
            A(S, "dve", "tensor_tensor", reads=[btabb, maskb], writes=[btabb], out=btab[:, mm, 0, :], in0=btab[:, mm, 0, :], in1=mask[:], op=ALU.add)
        A(S, "dve", "tensor_copy", reads=[btabb], writes=[btb], out=bt[:], in_=btab[:])
        for i in range(2):
            A(S, "dve", "memset", writes=[VPb[i]], ap=VP[i][:, :, 128:VW], constant=1.0)

        vv = dr["v"].rearrange("(j p) e -> p j e", p=128)

        def load_head(h):
            hs = h % 2
            load(S, KT[hs][:], dr["kT"][128 * h:128 * h + 128, :], KTb[hs])
            load(S, VP[hs][:, :, 0:128], vv[:, :, 128 * h:128 * h + 128], VPb[hs])

        def load_q(h, I, sl_):
            load(S, QT[sl_][:], dr["qT"][128 * h:128 * h + 128, I * TT:(I + 1) * TT], QTb[sl_])

        load_head(0)
        qcnt = 0
        load_q(0, 0, 0)
        jc = 0
        fcnt = 0
        for h in range(nheads):
            hs = h % 2
            if h + 1 < nheads:
                load_head(h + 1)
            for I in range(nqt):
                qsl = qcnt % 2
                qcnt += 1
                if I + 1 < nqt:
                    load_q(h, I + 1, 1 - qsl)
                elif h + 1 < nheads:
                    load_q(h + 1, 0, 1 - qsl)
                nkb = 4 * I + 4
                for bk in range(3):
                    A(S, "dve", "memset", writes=[pAb[a_] for a_ in range(3 * bk, min(8, 3 * bk + 3))], ap=pA[bk][:, 0:387], constant=0.0)
                slots = {}

                def emit_S(j):
                    nonlocal jc
                    a = j - 4 * I
                    qmin = max(0, a)
                    c0 = 128 * qmin
                    sl_ = jc % 2
                    jc += 1
                    slots[j] = sl_
                    for m in range(2):
                        mp = 2 * h + m
                        A(S, "pe", "matmul", reads=[KTb[hs], QTb[qsl]], writes=[pSb[sl_]], out=pS[sl_][:, m, c0:TT],
                          lhsT=KT[hs][64 * m:64 * m + 64, 128 * j:128 * j + 128], rhs=QT[qsl][64 * m:64 * m + 64, c0:TT], start=True, stop=True,
                          skip_group_check=True)
                        for qs in range(qmin, 4):
                            dl = 4 * I + qs - j
                            if dl in (0, 1):
                                A(S, "pe", "matmul", reads=[identbb, btb], writes=[pSb[sl_]], out=pS[sl_][:, m, 128 * qs:128 * qs + 128],
                                  lhsT=identb[:], rhs=bt[:, mp, dl, :], start=False, stop=True, skip_group_check=True)
                    A(S, "act", "activation", reads=[pSb[sl_]], writes=[PTb[sl_]], out=PT[sl_][:, :, c0:TT], in_=pS[sl_][:, :, c0:TT], func=AF.Exp)

                def emit_PV(j):
                    qmin = max(0, j - 4 * I)
                    sl_ = slots[j]
                    for m in range(2):
                        for qs in range(qmin, 4):
                            ap_, ab_ = acc(m, qs)
                            A(S, "pe", "matmul", reads=[PTb[sl_], VPb[hs]], writes=[ab_], out=ap_, lhsT=PT[sl_][:, m, 128 * qs:128 * qs + 128],
                              rhs=VP[hs][:, j, 0:129], start=False, stop=(j == 4 * I + qs), skip_group_check=True)

                emit_S(0)
                for j in range(nkb):
                    if j + 1 < nkb:
                        emit_S(j + 1)
                    emit_PV(j)
                asl = (h * nqt + I) % 2
                for qs in range(4):
                    f = fcnt % 2
                    fcnt += 1
                    a0, a0b = acc(0, qs)
                    a1, a1b = acc(1, qs)
                    A(S, "act", "copy", reads=[a0b], writes=[osbb[f][0]], out=osb[:, f, 0, 0:129], in_=a0)
                    A(S, "dve", "tensor_copy", reads=[a1b], writes=[osbb[f][1]], out=osb[:, f, 1, 0:129], in_=a1)
                    A(S, "dve", "reciprocal", reads=osbb[f], writes=[rrb[f]], out=rr[:, f, 0:2], in_=osb[:, f, :, 128])
                    A(S, "dve", "tensor_tensor", reads=[rrb[f], lsb], writes=[rrb[f]], out=rr[:, f, 2:3], in0=rr[:, f, 1:2], in1=lamc, op=ALU.mult)
                    A(S, "dve", "tensor_scalar", reads=[osbb[f][1], rrb[f]], writes=[t2b[f]], out=t2[:, f, :], in0=osb[:, f, 1, 0:128], scalar1=rr[:, f, 2:3], scalar2=None,
                      op0=ALU.mult)
                    A(S, "dve", "scalar_tensor_tensor", reads=[osbb[f][0], rrb[f], t2b[f]], writes=[ob[f]], out=o[:, f, :], in0=osb[:, f, 0, 0:128], scalar=rr[:, f, 0:1],
                      in1=t2[:, f, :], op0=ALU.mult, op1=ALU.subtract)
                    A(S, "act", "activation", reads=[ob[f]], writes=[junkb[f], rrb[f]], out=junk[:, f, :], in_=o[:, f, :], func=AF.Square, accum_out=rr[:, f, 3:4])
                    A(S, "act", "activation", reads=[rrb[f], epsb], writes=[rrb[f]], out=rr[:, f, 3:4], in_=rr[:, f, 3:4], func=AF.Sqrt, bias=epsT[:, 0:1], scale=1.0 / 128)
                    A(S, "dve", "reciprocal", reads=[rrb[f]], writes=[rrb[f]], out=rr[:, f, 3:4], in_=rr[:, f, 3:4])
                    A(S, "dve", "scalar_tensor_tensor", reads=[ob[f], rrb[f], gsubb], writes=[onb[f]], out=on[:, f, :], in0=o[:, f, :], scalar=rr[:, f, 3:4],
                      in1=gsub[:], op0=ALU.mult, op1=ALU.mult)
                    A(S, "pe", "transpose", reads=[onb[f], identbb], writes=[pTb], out=pT[:], in_=on[:, f, :], identity=identb[:])
                    A(S, "dve", "tensor_copy", reads=[pTb], writes=[aob[asl][qs]], out=ao[asl][:, 128 * qs:128 * qs + 128], in_=pT[:])
                store(S, dr["aT"][128 * h:128 * h + 128, I * TT:(I + 1) * TT], ao[asl][:], aob[asl])
        S.flush()


def make_dram(nc, debug):
    kind = "ExternalOutput" if debug else "Internal"
    dr = {}

    def mk(name, shape, dt=BF16):
        dr[name] = nc.dram_tensor(name, list(shape), dt, kind=kind).ap()

    mk("hT", [D, T]); mk("uT", [512, T]); mk("gT", [512, T]); mk("qT", [D, T]); mk("kT", [D, T])
    mk("v", [T, D]); mk("gatesT", [3 * D, T])
    mk("yaT", [D, T]); mk("ybT", [D, T]); mk("ycT", [D, T]); mk("aT", [D, T])
    mk("h2T", [D, T]); mk("fT", [DFF, T])
    mk("ssm_wb", [128, 2, 2048]); mk("ssm_rcol", [128, 16], F32); mk("ssm_tab", [128, 16, 2, 513], F32)
    return dr


ALL_PHASES = ["p1a", "p1b", "ssm_setup", "ssm", "conv", "attn", "attn_out", "merge", "ffn_in", "ffn_out"]


def build(shapes, n_layers=DEPTH, phases=None, debug=False, ntiles=NT, nheads=8):
    nc = bass.Bass("TRN2", target_bir_lowering=False)
    inp = {}
    for name, shp in shapes.items():
        inp[name] = nc.dram_tensor(name, list(shp), F32, kind="ExternalInput").ap()
    outT = nc.dram_tensor("outT", [D, T], F32, kind="ExternalOutput").ap()
    dr = make_dram(nc, debug)
    S = Sched(nc)
    for L in range(n_layers):
        xsrc = inp["xT"] if L == 0 else outT
        lam_init = 0.8 - 0.6 * math.exp(-0.3 * L)
        for ph in (phases or ALL_PHASES):
            if ph == "p1a":
                phase_p1a(S, L, xsrc, dr, inp, ntiles)
            elif ph == "p1b":
                phase_p1b(S, L, dr, inp, ntiles)
            elif ph == "ssm_setup":
                phase_ssm_setup(S, L, dr, inp)
            elif ph == "ssm":
                phase_ssm(S, L, dr, inp, ntiles)
            elif ph == "conv":
                phase_conv(S, L, dr, inp, ntiles)
            elif ph == "attn":
                phase_attn(S, L, dr, inp, lam_init, nheads, ntiles)
            elif ph == "attn_out":
                phase_linear(S, dr["aT"], dr["ycT"], inp["w_attn_out"][L], 8, 8, ntiles)
            elif ph == "merge":
                phase_merge(S, L, xsrc, outT, dr, inp, ntiles)
            elif ph == "ffn_in":
                phase_ffn_in(S, L, dr, inp, ntiles)
            elif ph == "ffn_out":
                phase_ffn_out(S, L, outT, dr, inp, ntiles)
    return nc


def _cols(v, n):
    Ld = v.shape[0]
    return np.ascontiguousarray(v.reshape(Ld, n, 128).transpose(0, 2, 1)).astype(np.float32)


def _rel_bucket(n):
    n = np.maximum(n, 0)
    nf = np.maximum(n, 16).astype(np.float32)
    large = 16 + (np.log(nf / np.float32(16)) / np.float32(math.log(128 / 16)) * np.float32(16)).astype(np.int32)
    large = np.minimum(large, 31)
    return np.where(n < 16, n, large)


def prep_shared(inputs):
    f32 = np.float32
    Ld = DEPTH
    sh = {}
    for k in ["w_in", "w_ssm_glu", "w_ssm_out", "w_conv_out", "w_attn_out", "w_out", "w_ffn_in", "w_ffn_out"]:
        sh[k] = np.ascontiguousarray(inputs[k], dtype=f32)
    sh["g_pre_mix"] = _cols(inputs["pre_mix_g"], 8)
    sh["g_post_mix"] = _cols(inputs["post_mix_g"], 8)
    sh["g_pre_ffn"] = _cols(inputs["pre_ffn_g"], 8)
    sh["g_post_ffn"] = _cols(inputs["post_ffn_g"], 8)
    dw = inputs["conv_dw"]
    sh["conv_dw_c"] = np.ascontiguousarray(dw.reshape(Ld, 31, 4, 128).transpose(0, 3, 2, 1)).astype(f32)
    prm = np.stack([_cols(inputs["conv_dw_b"], 4), _cols(inputs["conv_ln_g"], 4), _cols(inputs["conv_ln_b"], 4)], axis=2)
    sh["conv_prm"] = np.ascontiguousarray(prm).astype(f32)
    sh["ident"] = np.eye(128, dtype=f32)
    sh["iota"] = np.ascontiguousarray(np.broadcast_to(np.arange(513, dtype=f32), (128, 513)))
    are = inputs["ssm_a_re"].reshape(Ld, 2048); aim = inputs["ssm_a_im"].reshape(Ld, 2048)
    ldt = np.repeat(inputs["ssm_log_dt"], 64, axis=1).reshape(Ld, 2048)
    rows = np.stack([are, aim, ldt], axis=1)
    sh["ssm_rows"] = np.ascontiguousarray(np.broadcast_to(rows[:, None], (Ld, 128, 3, 2048))).astype(f32)
    sh["ssm_cols"] = np.ascontiguousarray(rows.reshape(Ld, 3, 16, 128).transpose(0, 3, 1, 2)).astype(f32)
    bblk = np.zeros((Ld, 128, 2, 16, 128), f32)
    cblk = np.zeros((Ld, 128, 2, 16, 128), f32)
    for g in range(32):
        i = g // 2
        po = 64 * (g % 2)
        co = 16 * (g % 8)
        for ri, (bn, cn) in enumerate((("ssm_b_re", "ssm_c_re"), ("ssm_b_im", "ssm_c_im"))):
            bblk[:, co:co + 16, ri, i, po:po + 64] = inputs[bn][:, g].transpose(0, 2, 1)
            cblk[:, po:po + 64, ri, i, co:co + 16] = inputs[cn][:, g].transpose(0, 2, 1)
    sh["ssm_bblk"] = bblk.reshape(Ld, 128, 2, 2048)
    sh["ssm_cblk"] = cblk.reshape(Ld, 128, 2, 2048)
    sh["ssm_dcol"] = _cols(inputs["ssm_d"].reshape(Ld, 512), 4)
    rb = np.asarray(inputs["rel_bias"], f32)
    kk = np.arange(128)[:, None]; qq = np.arange(128)[None, :]
    bt = np.zeros((128, 16, 2, 128), f32)
    for dl in range(2):
        bk = _rel_bucket(qq - kk + 128 * dl)
        bt[:, :, dl, :] = rb[bk].transpose(0, 2, 1)
    sh["att_btab"] = bt
    sh["att_mask"] = np.where(qq >= kk, 0.0, -30000.0).astype(f32)
    sh["att_c31"] = np.ascontiguousarray(np.broadcast_to(rb[31][None, :], (128, 16))).astype(f32)
    lv = np.stack([inputs["lambda_q1"], inputs["lambda_k1"], inputs["lambda_q2"], inputs["lambda_k2"]], axis=1)
    sh["att_lamv"] = np.ascontiguousarray(np.broadcast_to(lv[:, None], (Ld, 128, 4, 64))).astype(f32)
    sh["att_gsub"] = np.ascontiguousarray(np.broadcast_to(inputs["attn_subln_g"][:, None, :], (Ld, 128, 128))).astype(f32)
    return sh


_NC_CACHE = {}


def kernel(**inputs):
    inputs = {k: np.asarray(v) for k, v in inputs.items()}
    sh = prep_shared(inputs)
    x = inputs["x"]
    shapes = {"xT": (D, T)}
    shapes.update({k: v.shape for k, v in sh.items()})
    if "nc" not in _NC_CACHE:
        _NC_CACHE["nc"] = build(shapes)
    nc = _NC_CACHE["nc"]
    in_maps = []
    for c in range(8):
        m = dict(sh)
        m["xT"] = np.ascontiguousarray(x[c // 2].T)
        in_maps.append(m)
    res = run_bass_kernel_spmd(nc, in_maps, core_ids=list(range(8)))
    out = np.stack([np.ascontiguousarray(res.results[2 * b]["outT"].T) for b in range(4)], axis=0)
    return out.astype(np.float32)
```

```python
import math
from contextlib import ExitStack
import numpy as np
import concourse.bass as bass
import concourse.mybir as mybir
from concourse.bass_utils import run_bass_kernel_spmd

F32 = mybir.dt.float32
BF16 = mybir.dt.bfloat16
ALU = mybir.AluOpType
AF = mybir.ActivationFunctionType

T = 8192
D = 1024
TT = 512
NT = T // TT
DEPTH = 4
DFF = 2816
NFC = DFF // 128
SAME_ENGINE_SYNC = True


class Buf:
    __slots__ = ("name", "lw", "rd")

    def __init__(self, name):
        self.name = name
        self.lw = None
        self.rd = []


def bufs(name, n):
    return [Buf(f"{name}{i}") for i in range(n)]


class Sched:
    CE = ["act", "pe", "dve", "pool"]

    def __init__(self, nc, n_dma_sems=40):
        self.nc = nc
        self.csem = {e: nc.alloc_semaphore(name=f"c_{e}") for e in self.CE}
        self.ccnt = {e: 0 for e in self.CE}
        self.dsems = [nc.alloc_semaphore(name=f"dm{i}") for i in range(n_dma_sems)]
        self.dcnt = [0] * n_dma_sems
        self.es = None
        self.reset()

    def reset(self):
        self.lists = {e: [] for e in ["sync", "act", "pe", "dve", "pool"]}
        self.seen = {}
        self.dkey = {}
        self.final = {e: {} for e in self.lists}
        self.ninst = 0

    def sb(self, name, shape, dtype):
        self.uid = getattr(self, "uid", 0) + 1
        return self.es.enter_context(self.nc.sbuf_tensor(f"{name}_{self.uid}", shape, dtype))

    def ps(self, name, shape=(128, 512), dtype=F32):
        self.uid = getattr(self, "uid", 0) + 1
        return self.es.enter_context(self.nc.psum_tensor(f"{name}_{self.uid}", list(shape), dtype))

    def add(self, eng, op, kw, reads=(), writes=(), dma=False):
        deps = []
        for b in reads:
            if b.lw is not None:
                deps.append(b.lw)
        for b in writes:
            if b.lw is not None:
                deps.append(b.lw)
            deps.extend(b.rd)
        mykey = None
        if dma:
            kb = (writes[0] if writes else reads[0]).name + ("_w" if writes else "_r")
            if kb not in self.dkey:
                self.dkey[kb] = len(self.dkey)
                assert len(self.dkey) <= len(self.dsems), "out of dma sems"
            mykey = self.dkey[kb]
        waits = []
        for tok in deps:
            kind, key, val = tok
            if kind == "d" and dma and key == mykey:
                continue
            if kind == "c":
                if key == eng and (eng == "pe" or not SAME_ENGINE_SYNC):
                    continue
                sk = (eng, "c", key)
                sem = self.csem[key]
            else:
                sk = (eng, "d", key)
                sem = self.dsems[key]
            if self.seen.get(sk, 0) >= val:
                continue
            self.seen[sk] = val
            waits.append((sem, val))
        if dma:
            di = mykey
            self.dcnt[di] += 16
            tok = ("d", di, self.dcnt[di])
            inc = (self.dsems[di], 16)
            self.final[eng][di] = self.dcnt[di]
        else:
            self.ccnt[eng] += 1
            tok = ("c", eng, self.ccnt[eng])
            inc = (self.csem[eng], 1)
        self.lists[eng].append((waits, (op, kw), inc))
        for b in reads:
            b.rd.append(tok)
        for b in writes:
            b.lw = tok
            b.rd = []
        self.ninst += 1
        return tok

    def flush(self):
        nc = self.nc
        with nc.Block() as block:
            for en, attr in (("sync", block.sync), ("act", block.scalar), ("pe", block.tensor),
                             ("dve", block.vector), ("pool", block.gpsimd)):
                lst = self.lists[en]
                fin = self.final[en]

                def body(eng, lst=lst, fin=fin):
                    for waits, fn, inc in lst:
                        for sem, val in waits:
                            eng.wait_ge(sem, val)
                        ins = getattr(eng, fn[0])(**fn[1])
                        ins.then_inc(inc[0], inc[1])
                    for di, val in fin.items():
                        eng.wait_ge(self.dsems[di], val)

                attr(body)
        self.reset()


def A(S, eng, _opname, reads=(), writes=(), dma=False, **kw):
    return S.add(eng, _opname, kw, reads, writes, dma)


def fm(dram, c0, c1, t0, n=TT):
    return dram.rearrange("(c p) t -> p c t", p=128)[:, c0:c1, t0:t0 + n]


def rms_rstd(S, sq_aps, sq_bufs, ones, ps_ss, ps_ss_b, rs, rs_b, rstd, rstd_b, eps_ap, inv_n, const_bufs=()):
    n = len(sq_aps)
    cb = list(const_bufs)
    for kc in range(n):
        A(S, "pe", "matmul", reads=[sq_bufs[kc]] + cb, writes=[ps_ss_b], out=ps_ss, lhsT=ones, rhs=sq_aps[kc], start=(kc == 0), stop=(kc == n - 1))
    A(S, "act", "activation", reads=[ps_ss_b] + cb, writes=[rs_b], out=rs, in_=ps_ss, func=AF.Sqrt, bias=eps_ap, scale=inv_n)
    A(S, "dve", "reciprocal", reads=[rs_b], writes=[rstd_b], out=rstd, in_=rs)


def load_w_cast(S, Wt, Wb, src2d, kcn, ncols, col0=0, maxc=2048):
    v = src2d.rearrange("(kc p) n -> p kc n", p=128)
    for kc in range(kcn):
        c = 0
        while c < ncols:
            w = min(maxc, ncols - c)
            A(S, "pool", "dma_start", writes=[Wb], dma=True, out=Wt[:, kc, c:c + w], in_=v[:, kc, col0 + c:col0 + c + w])
            c += w


def load(S, dst_ap, src_ap, b):
    A(S, "sync", "dma_start", writes=[b] if not isinstance(b, list) else b, dma=True, out=dst_ap, in_=src_ap)


def store(S, dst_ap, src_ap, rb):
    A(S, "sync", "dma_start", reads=rb if isinstance(rb, list) else [rb], dma=True, out=dst_ap, in_=src_ap)


class PsRot:
    def __init__(self, S, n, name="ps"):
        self.t = [S.ps(f"{name}{i}") for i in range(n)]
        self.b = bufs(name, n)
        self.i = 0
        self.n = n

    def next(self):
        k = self.i % self.n
        self.i += 1
        return self.t[k], self.b[k]


def mm_group(S, pr, lhs_fn, rhs_fn, kcn, reads_fn):
    pt, pb = pr.next()
    for kc in range(kcn):
        A(S, "pe", "matmul", reads=reads_fn(kc), writes=[pb], out=pt[:], lhsT=lhs_fn(kc), rhs=rhs_fn(kc), start=(kc == 0), stop=(kc == kcn - 1))
    return pt, pb


def phase_p1a(S, L, xsrc, dr, inp, ntiles=NT):
    with ExitStack() as es:
        S.es = es
        W = S.sb("W1a", [128, 8, 3584], BF16); Wb = Buf("W1a")
        xt = [S.sb(f"x{i}", [128, 8, TT], F32) for i in range(2)]; xb = bufs("x", 2)
        sq = S.sb("sq", [128, 8, TT], BF16); sqb = bufs("sq", 8)
        h = [S.sb(f"h{i}", [128, 8, TT], BF16) for i in range(2)]; hb = [bufs(f"h{i}_", 8) for i in range(2)]
        rs = S.sb("rs", [128, TT], F32); rsb = Buf("rs")
        rstd = S.sb("rstd", [128, TT], F32); rstdb = Buf("rstd")
        gcol = S.sb("gcol", [128, 8], F32); gcb = Buf("gcol")
        ones = S.sb("ones", [128, 128], BF16); onesb = Buf("ones")
        epsT = S.sb("eps", [128, 1], F32); epsb = Buf("eps")
        uo = S.sb("uo", [128, 4, TT], BF16); uob = bufs("uo", 4)
        sg = S.sb("sg", [128, 4, TT], BF16); sgb = bufs("sg", 4)
        go = S.sb("go", [128, 4, TT], BF16); gob = bufs("go", 4)
        qo = S.sb("qo", [128, 8, TT], BF16); qob = bufs("qo", 8)
        ko = S.sb("ko", [128, 8, TT], BF16); kob = bufs("ko", 8)
        pr = PsRot(S, 6)
        pss = S.ps("pss"); pssb = Buf("pss")

        A(S, "dve", "memset", writes=[onesb], ap=ones[:], constant=1.0)
        A(S, "dve", "memset", writes=[epsb], ap=epsT[:], constant=1e-6)
        load(S, gcol[:], inp["g_pre_mix"][L], gcb)
        load_w_cast(S, W, Wb, inp["w_in"][L], 8, 3584, 0, maxc=1792)

        load(S, xt[0][:], fm(xsrc, 0, 8, 0), xb[0])
        for tt in range(ntiles):
            s = tt % 2
            t0 = tt * TT
            if tt + 1 < ntiles:
                load(S, xt[1 - s][:], fm(xsrc, 0, 8, t0 + TT), xb[1 - s])
            for kc in range(8):
                A(S, "act", "activation", reads=[xb[s]], writes=[sqb[kc]], out=sq[:, kc, :], in_=xt[s][:, kc, :], func=AF.Square)
            rms_rstd(S, [sq[:, kc, :] for kc in range(8)], sqb, ones[:], pss[:], pssb, rs[:], rsb, rstd[:], rstdb, epsT[:, 0:1], 1.0 / D, const_bufs=(onesb, epsb))
            for kc in range(8):
                A(S, "dve", "scalar_tensor_tensor", reads=[xb[s], gcb, rstdb], writes=[hb[s][kc]], out=h[s][:, kc, :], in0=xt[s][:, kc, :],
                  scalar=gcol[:, kc:kc + 1], in1=rstd[:], op0=ALU.mult, op1=ALU.mult)
            store(S, fm(dr["hT"], 0, 8, t0), h[s][:], hb[s])

            def group(cc):
                return mm_group(S, pr, lambda kc: W[:, kc, cc * 128:(cc + 1) * 128], lambda kc: h[s][:, kc, :], 8, lambda kc: [Wb, hb[s][kc]])

            for c in range(4):
                pt, pb = group(c)
                A(S, "dve", "tensor_copy", reads=[pb], writes=[uob[c]], out=uo[:, c, :], in_=pt[:])
            store(S, fm(dr["uT"], 0, 4, t0), uo[:], uob)
            for c in range(4):
                pt, pb = group(8 + c)
                A(S, "act", "activation", reads=[pb], writes=[sgb[c]], out=sg[:, c, :], in_=pt[:], func=AF.Sigmoid)
            for c in range(4):
                pt, pb = group(4 + c)
                A(S, "dve", "tensor_tensor", reads=[pb, sgb[c]], writes=[gob[c]], out=go[:, c, :], in0=pt[:], in1=sg[:, c, :], op=ALU.mult)
            store(S, fm(dr["gT"], 0, 4, t0), go[:], gob)
            for c in range(8):
                pt, pb = group(12 + c)
                A(S, "act", "mul", reads=[pb], writes=[qob[c]], out=qo[:, c, :], in_=pt[:], mul=0.125)
            store(S, fm(dr["qT"], 0, 8, t0), qo[:], qob)
            for c in range(8):
                pt, pb = group(20 + c)
                A(S, "dve", "tensor_copy", reads=[pb], writes=[kob[c]], out=ko[:, c, :], in_=pt[:])
            store(S, fm(dr["kT"], 0, 8, t0), ko[:], kob)
        S.flush()


def phase_p1b(S, L, dr, inp, ntiles=NT):
    with ExitStack() as es:
        S.es = es
        W = S.sb("W1b", [128, 8, 4096], BF16); Wb = Buf("W1b")
        h = [S.sb(f"h{i}", [128, 8, TT], BF16) for i in range(2)]; hb = bufs("h", 2)
        vo = S.sb("vo", [128, 4, 1024], BF16); vob = bufs("vo", 8)
        go = [S.sb(f"go{i}", [128, 8, TT], BF16) for i in range(2)]; gob = [bufs(f"go{i}_", 8) for i in range(2)]
        pr = PsRot(S, 7)
        load_w_cast(S, W, Wb, inp["w_in"][L], 8, 4096, 3584)
        load(S, h[0][:], fm(dr["hT"], 0, 8, 0), hb[0])
        for tt in range(ntiles):
            s = tt % 2
            t0 = tt * TT
            if tt + 1 < ntiles:
                load(S, h[1 - s][:], fm(dr["hT"], 0, 8, t0 + TT), hb[1 - s])
            for tb in range(4):
                for ch in range(2):
                    pt, pb = mm_group(S, pr, lambda kc: h[s][:, kc, tb * 128:(tb + 1) * 128], lambda kc: W[:, kc, ch * 512:(ch + 1) * 512], 8,
                                      lambda kc: [Wb, hb[s]])
                    A(S, "dve", "tensor_copy", reads=[pb], writes=[vob[tb * 2 + ch]], out=vo[:, tb, ch * 512:(ch + 1) * 512], in_=pt[:])
            store(S, dr["v"].rearrange("(j p) e -> p j e", p=128)[:, 4 * tt:4 * tt + 4, :], vo[:], vob)
            for gi in range(3):
                gs = gi % 2
                for c in range(8):
                    cc = 8 + gi * 8 + c
                    pt, pb = mm_group(S, pr, lambda kc: W[:, kc, cc * 128:(cc + 1) * 128], lambda kc: h[s][:, kc, :], 8, lambda kc: [Wb, hb[s]])
                    A(S, "act", "activation", reads=[pb], writes=[gob[gs][c]], out=go[gs][:, c, :], in_=pt[:], func=AF.Sigmoid)
                store(S, fm(dr["gatesT"], gi * 8, gi * 8 + 8, t0), go[gs][:], gob[gs])
        S.flush()
def phase_conv(S, L, dr, inp, ntiles=NT):
    HW = TT + 30
    with ExitStack() as es:
        S.es = es
        W = S.sb("Wc", [128, 4, 1024], BF16); Wb = Buf("Wc")
        ident = S.sb("ident", [128, 128], F32); idb = Buf("ident")
        dwc = S.sb("dwc", [128, 4, 31], F32); dwb = Buf("dwc")
        prm = S.sb("prm", [128, 3, 4], F32); prmb = Buf("prm")
        diag = S.sb("diag", [128, 124, 128], BF16); dgb = bufs("dg", 124)
        ones = S.sb("ones", [128, 128], BF16); onesb = Buf("ones")
        epsT = S.sb("eps", [128, 1], F32); epsb = Buf("eps")
        gt = [S.sb(f"gt{i}", [128, 4, HW], BF16) for i in range(2)]; gtb = bufs("gt", 2)
        y = S.sb("y", [128, 4, TT], F32); yb = bufs("y", 4)
        ybf = S.sb("ybf", [128, 4, TT], BF16); ybfb = bufs("ybf", 4)
        ysq = S.sb("ysq", [128, 4, TT], BF16); ysqb = bufs("ysq", 4)
        mean = S.sb("mean", [128, TT], F32); meanb = Buf("mean")
        m2 = S.sb("m2", [128, TT], F32); m2b = Buf("m2")
        var = S.sb("var", [128, TT], F32); varb = Buf("var")
        rs = S.sb("rs", [128, TT], F32); rsb = Buf("rs")
        rstd = S.sb("rstd", [128, TT], F32); rstdb = Buf("rstd")
        t1 = S.sb("t1", [128, 4, TT], F32); t1b = bufs("t1", 4)
        t2 = S.sb("t2", [128, 4, TT], F32); t2b = bufs("t2", 4)
        z = S.sb("z", [128, 4, TT], BF16); zb = bufs("z", 4)
        yo = S.sb("yo", [128, 8, TT], BF16); yob = bufs("yo", 8)
        pr = PsRot(S, 5)
        psm = S.ps("psm"); psmb = Buf("psm")
        psq = S.ps("psq"); psqb = Buf("psq")

        A(S, "dve", "memset", writes=[onesb], ap=ones[:], constant=1.0)
        A(S, "dve", "memset", writes=[epsb], ap=epsT[:], constant=1e-5)
        load(S, ident[:], inp["ident"], idb)
        load(S, dwc[:], inp["conv_dw_c"][L], dwb)
        load(S, prm[:], inp["conv_prm"][L], prmb)
        load_w_cast(S, W, Wb, inp["w_conv_out"][L], 4, 1024)
        for c in range(4):
            for k in range(31):
                A(S, "dve", "tensor_scalar", reads=[idb, dwb], writes=[dgb[c * 31 + k]], out=diag[:, c * 31 + k, :], in0=ident[:],
                  scalar1=dwc[:, c, k:k + 1], scalar2=None, op0=ALU.mult)
        A(S, "dve", "memset", writes=[gtb[0]], ap=gt[0][:, :, 0:30], constant=0.0)
        load(S, gt[0][:, :, 30:HW], fm(dr["gT"], 0, 4, 0), gtb[0])
        for tt in range(ntiles):
            s = tt % 2
            t0 = tt * TT
            if tt + 1 < ntiles:
                load(S, gt[1 - s][:], fm(dr["gT"], 0, 4, t0 + TT - 30, HW), gtb[1 - s])
            for c in range(4):
                pt, pb = pr.next()
                for k in range(31):
                    A(S, "pe", "matmul", reads=[dgb[c * 31 + k], gtb[s]], writes=[pb], out=pt[:], lhsT=diag[:, c * 31 + k, :], rhs=gt[s][:, c, k:k + TT],
                      start=(k == 0), stop=(k == 30))
                A(S, "act", "activation", reads=[pb, prmb], writes=[yb[c]], out=y[:, c, :], in_=pt[:], func=AF.Identity, bias=prm[:, 0, c:c + 1], scale=1.0)
                A(S, "dve", "tensor_copy", reads=[yb[c]], writes=[ybfb[c]], out=ybf[:, c, :], in_=y[:, c, :])
                A(S, "act", "activation", reads=[yb[c]], writes=[ysqb[c]], out=ysq[:, c, :], in_=y[:, c, :], func=AF.Square)
            for c in range(4):
                A(S, "pe", "matmul", reads=[ybfb[c], onesb], writes=[psmb], out=psm[:], lhsT=ones[:], rhs=ybf[:, c, :], start=(c == 0), stop=(c == 3))
            for c in range(4):
                A(S, "pe", "matmul", reads=[ysqb[c], onesb], writes=[psqb], out=psq[:], lhsT=ones[:], rhs=ysq[:, c, :], start=(c == 0), stop=(c == 3))
            A(S, "act", "mul", reads=[psmb], writes=[meanb], out=mean[:], in_=psm[:], mul=1.0 / 512)
            A(S, "dve", "tensor_tensor", reads=[meanb], writes=[m2b], out=m2[:], in0=mean[:], in1=mean[:], op=ALU.mult)
            A(S, "dve", "scalar_tensor_tensor", reads=[psqb, m2b], writes=[varb], out=var[:], in0=psq[:], scalar=1.0 / 512, in1=m2[:],
              op0=ALU.mult, op1=ALU.subtract)
            A(S, "act", "activation", reads=[varb, epsb], writes=[rsb], out=rs[:], in_=var[:], func=AF.Sqrt, bias=epsT[:, 0:1], scale=1.0)
            A(S, "dve", "reciprocal", reads=[rsb], writes=[rstdb], out=rstd[:], in_=rs[:])
            for c in range(4):
                A(S, "pool", "tensor_tensor", reads=[yb[c], meanb], writes=[t1b[c]], out=t1[:, c, :], in0=y[:, c, :], in1=mean[:], op=ALU.subtract)
                A(S, "dve", "tensor_tensor", reads=[t1b[c], rstdb], writes=[t2b[c]], out=t2[:, c, :], in0=t1[:, c, :], in1=rstd[:], op=ALU.mult)
                A(S, "act", "activation", reads=[t2b[c], prmb], writes=[zb[c]], out=z[:, c, :], in_=t2[:, c, :], func=AF.Silu,
                  bias=prm[:, 2, c:c + 1], scale=prm[:, 1, c:c + 1])
            for cc in range(8):
                pt, pb = mm_group(S, pr, lambda kc: W[:, kc, cc * 128:(cc + 1) * 128], lambda kc: z[:, kc, :], 4, lambda kc: [Wb, zb[kc]])
                A(S, "dve" if cc % 2 else "act", "tensor_copy" if cc % 2 else "copy", reads=[pb], writes=[yob[cc]], out=yo[:, cc, :], in_=pt[:])
            store(S, fm(dr["ybT"], 0, 8, t0), yo[:], yob)
        S.flush()


def phase_linear(S, src, dst, wsrc, kcn, ncc, ntiles=NT):
    with ExitStack() as es:
        S.es = es
        W = S.sb("Wl", [128, kcn, ncc * 128], BF16); Wb = Buf("Wl")
        it = [S.sb(f"it{i}", [128, kcn, TT], BF16) for i in range(2)]; itb = bufs("it", 2)
        ot = [S.sb(f"ot{i}", [128, ncc, TT], BF16) for i in range(2)]; otb = [bufs(f"ot{i}_", ncc) for i in range(2)]
        pr = PsRot(S, 6)
        load_w_cast(S, W, Wb, wsrc, kcn, ncc * 128, maxc=1024)
        load(S, it[0][:], fm(src, 0, kcn, 0), itb[0])
        for tt in range(ntiles):
            s = tt % 2
            t0 = tt * TT
            if tt + 1 < ntiles:
                load(S, it[1 - s][:], fm(src, 0, kcn, t0 + TT), itb[1 - s])
            for cc in range(ncc):
                pt, pb = mm_group(S, pr, lambda kc: W[:, kc, cc * 128:(cc + 1) * 128], lambda kc: it[s][:, kc, :], kcn, lambda kc: [Wb, itb[s]])
                A(S, "dve" if cc % 2 else "act", "tensor_copy" if cc % 2 else "copy", reads=[pb], writes=[otb[s][cc]], out=ot[s][:, cc, :], in_=pt[:])
            store(S, fm(dst, 0, ncc, t0), ot[s][:], otb[s])
        S.flush()


def phase_merge(S, L, xsrc, outT, dr, inp, ntiles=NT):
    with ExitStack() as es:
        S.es = es
        W = S.sb("Wo", [128, 8, 1024], BF16); Wb = Buf("Wo")
        yt = [[S.sb(f"y{j}_{i}", [128, 8, TT], BF16) for i in range(2)] for j in range(3)]
        ytb = [bufs(f"y{j}_", 2) for j in range(3)]
        gt = S.sb("gt", [128, 24, TT], BF16); gtb = bufs("gt", 3)
        xt = [S.sb(f"x{i}", [128, 8, TT], F32) for i in range(2)]; xb = bufs("x", 2)
        ta = S.sb("ta", [128, 2, TT], F32); tab = bufs("ta", 2)
        tb_ = S.sb("tb", [128, 2, TT], F32); tbb = bufs("tb", 2)
        tc_ = S.sb("tc", [128, 2, TT], F32); tcb = bufs("tc", 2)
        m = S.sb("m", [128, 8, TT], BF16); mb = bufs("m", 8)
        osb = S.sb("osb", [128, 8, TT], F32); osbb = bufs("osb", 8)
        sq = S.sb("sq", [128, 8, TT], BF16); sqb = bufs("sq", 8)
        h2 = S.sb("h2", [128, 8, TT], BF16); h2b = bufs("h2", 8)
        rs = S.sb("rs", [128, TT], F32); rsb = Buf("rs")
        rstd = S.sb("rstd", [128, TT], F32); rstdb = Buf("rstd")
        gcol = S.sb("gcol", [128, 2, 8], F32); gcb = Buf("gcol")
        ones = S.sb("ones", [128, 128], BF16); onesb = Buf("ones")
        epsT = S.sb("eps", [128, 1], F32); epsb = Buf("eps")
        pr = PsRot(S, 6)
        pss = S.ps("pss"); pssb = Buf("pss")
        A(S, "dve", "memset", writes=[onesb], ap=ones[:], constant=1.0)
        A(S, "dve", "memset", writes=[epsb], ap=epsT[:], constant=1e-6)
        load(S, gcol[:, 0, :], inp["g_post_mix"][L], gcb)
        load(S, gcol[:, 1, :], inp["g_pre_ffn"][L], gcb)
        load_w_cast(S, W, Wb, inp["w_out"][L], 8, 1024, maxc=1024)
        ysrc = [dr["yaT"], dr["ybT"], dr["ycT"]]

        def loads(tt):
            s = tt % 2
            for j in range(3):
                load(S, yt[j][s][:], fm(ysrc[j], 0, 8, tt * TT), ytb[j][s])
            load(S, xt[s][:], fm(xsrc, 0, 8, tt * TT), xb[s])

        loads(0)
        for tt in range(ntiles):
            s = tt % 2
            t0 = tt * TT
            for j in range(3):
                load(S, gt[:, j * 8:(j + 1) * 8, :], fm(dr["gatesT"], j * 8, j * 8 + 8, t0), gtb[j])
            if tt + 1 < ntiles:
                loads(tt + 1)
            for kc in range(8):
                r = kc % 2
                A(S, "dve", "tensor_tensor", reads=[gtb[0], ytb[0][s]], writes=[tab[r]], out=ta[:, r, :], in0=gt[:, kc, :], in1=yt[0][s][:, kc, :], op=ALU.mult)
                A(S, "pool", "tensor_tensor", reads=[gtb[1], ytb[1][s]], writes=[tbb[r]], out=tb_[:, r, :], in0=gt[:, 8 + kc, :], in1=yt[1][s][:, kc, :], op=ALU.mult)
                A(S, "pool", "tensor_tensor", reads=[gtb[2], ytb[2][s]], writes=[tcb[r]], out=tc_[:, r, :], in0=gt[:, 16 + kc, :], in1=yt[2][s][:, kc, :], op=ALU.mult)
                A(S, "dve", "tensor_tensor", reads=[tab[r], tbb[r]], writes=[tab[r]], out=ta[:, r, :], in0=ta[:, r, :], in1=tb_[:, r, :], op=ALU.add)
                A(S, "dve", "tensor_tensor", reads=[tab[r], tcb[r]], writes=[mb[kc]], out=m[:, kc, :], in0=ta[:, r, :], in1=tc_[:, r, :], op=ALU.add)
            for cc in range(8):
                pt, pb = mm_group(S, pr, lambda kc: W[:, kc, cc * 128:(cc + 1) * 128], lambda kc: m[:, kc, :], 8, lambda kc: [Wb, mb[kc]])
                A(S, "act", "copy", reads=[pb], writes=[osbb[cc]], out=osb[:, cc, :], in_=pt[:])
                A(S, "act", "activation", reads=[pb], writes=[sqb[cc]], out=sq[:, cc, :], in_=pt[:], func=AF.Square)
            rms_rstd(S, [sq[:, kc, :] for kc in range(8)], sqb, ones[:], pss[:], pssb, rs[:], rsb, rstd[:], rstdb, epsT[:, 0:1], 1.0 / D, const_bufs=(onesb, epsb))
            for kc in range(8):
                A(S, "dve", "scalar_tensor_tensor", reads=[osbb[kc], gcb, rstdb], writes=[osbb[kc]], out=osb[:, kc, :], in0=osb[:, kc, :],
                  scalar=gcol[:, 0, kc:kc + 1], in1=rstd[:], op0=ALU.mult, op1=ALU.mult)
                A(S, "pool", "tensor_tensor", reads=[osbb[kc], xb[s]], writes=[osbb[kc]], out=osb[:, kc, :], in0=osb[:, kc, :], in1=xt[s][:, kc, :], op=ALU.add)
                A(S, "act", "activation", reads=[osbb[kc]], writes=[sqb[kc]], out=sq[:, kc, :], in_=osb[:, kc, :], func=AF.Square)
            store(S, fm(outT, 0, 8, t0), osb[:], osbb)
            rms_rstd(S, [sq[:, kc, :] for kc in range(8)], sqb, ones[:], pss[:], pssb, rs[:], rsb, rstd[:], rstdb, epsT[:, 0:1], 1.0 / D, const_bufs=(onesb, epsb))
            for kc in range(8):
                A(S, "dve", "scalar_tensor_tensor", reads=[osbb[kc], gcb, rstdb], writes=[h2b[kc]], out=h2[:, kc, :], in0=osb[:, kc, :],
                  scalar=gcol[:, 1, kc:kc + 1], in1=rstd[:], op0=ALU.mult, op1=ALU.mult)
            store(S, fm(dr["h2T"], 0, 8, t0), h2[:], h2b)
        S.flush()


def phase_ffn_in(S, L, dr, inp, ntiles=NT):
    with ExitStack() as es:
        S.es = es
        W = S.sb("Wf", [128, 8, 2 * DFF], BF16); Wb = Buf("Wf")
        it = [S.sb(f"it{i}", [128, 8, TT], BF16) for i in range(2)]; itb = bufs("it", 2)
        fo = S.sb("fo", [128, NFC, TT], BF16); fob = bufs("fo", NFC)
        sl = S.sb("sl", [128, 2, TT], BF16); slb = bufs("sl", 2)
        pr = PsRot(S, 7)
        load_w_cast(S, W, Wb, inp["w_ffn_in"][L], 8, 2 * DFF, maxc=1408)
        load(S, it[0][:], fm(dr["h2T"], 0, 8, 0), itb[0])
        for tt in range(ntiles):
            s = tt % 2
            t0 = tt * TT
            if tt + 1 < ntiles:
                load(S, it[1 - s][:], fm(dr["h2T"], 0, 8, t0 + TT), itb[1 - s])
            for j in range(NFC):
                r = j % 2
                pt, pb = mm_group(S, pr, lambda kc: W[:, kc, j * 128:(j + 1) * 128], lambda kc: it[s][:, kc, :], 8, lambda kc: [Wb, itb[s]])
                A(S, "act", "activation", reads=[pb], writes=[slb[r]], out=sl[:, r, :], in_=pt[:], func=AF.Silu)
                pt2, pb2 = mm_group(S, pr, lambda kc: W[:, kc, DFF + j * 128:DFF + (j + 1) * 128], lambda kc: it[s][:, kc, :], 8, lambda kc: [Wb, itb[s]])
                A(S, "dve", "tensor_tensor", reads=[pb2, slb[r]], writes=[fob[j]], out=fo[:, j, :], in0=pt2[:], in1=sl[:, r, :], op=ALU.mult)
            store(S, fm(dr["fT"], 0, 11, t0), fo[:, 0:11, :], fob[0:11])
            store(S, fm(dr["fT"], 11, 22, t0), fo[:, 11:22, :], fob[11:22])
        S.flush()


def phase_ffn_out(S, L, outT, dr, inp, ntiles=NT):
    with ExitStack() as es:
        S.es = es
        W = S.sb("Wfo", [128, NFC, 1024], BF16); Wb = Buf("Wfo")
        it = [S.sb(f"it{i}", [128, NFC, TT], BF16) for i in range(2)]; itb = bufs("it", 2)
        xt = [S.sb(f"x{i}", [128, 8, TT], F32) for i in range(2)]; xb = bufs("x", 2)
        osb = S.sb("osb", [128, 8, TT], F32); osbb = bufs("osb", 8)
        sq = S.sb("sq", [128, 8, TT], BF16); sqb = bufs("sq", 8)
        rs = S.sb("rs", [128, TT], F32); rsb = Buf("rs")
        rstd = S.sb("rstd", [128, TT], F32); rstdb = Buf("rstd")
        gcol = S.sb("gcol", [128, 8], F32); gcb = Buf("gcol")
        ones = S.sb("ones", [128, 128], BF16); onesb = Buf("ones")
        epsT = S.sb("eps", [128, 1], F32); epsb = Buf("eps")
        pr = PsRot(S, 6)
        pss = S.ps("pss"); pssb = Buf("pss")
        A(S, "dve", "memset", writes=[onesb], ap=ones[:], constant=1.0)
        A(S, "dve", "memset", writes=[epsb], ap=epsT[:], constant=1e-6)
        load(S, gcol[:], inp["g_post_ffn"][L], gcb)
        load_w_cast(S, W, Wb, inp["w_ffn_out"][L], NFC, 1024, maxc=1024)

        def loads(tt):
            s = tt % 2
            load(S, it[s][:], fm(dr["fT"], 0, NFC, tt * TT), itb[s])
            load(S, xt[s][:], fm(outT, 0, 8, tt * TT), xb[s])

        loads(0)
        for tt in range(ntiles):
            s = tt % 2
            t0 = tt * TT
            if tt + 1 < ntiles:
                loads(tt + 1)
            for cc in range(8):
                pt, pb = mm_group(S, pr, lambda kc: W[:, kc, cc * 128:(cc + 1) * 128], lambda kc: it[s][:, kc, :], NFC, lambda kc: [Wb, itb[s]])
                A(S, "act", "copy", reads=[pb], writes=[osbb[cc]], out=osb[:, cc, :], in_=pt[:])
                A(S, "act", "activation", reads=[pb], writes=[sqb[cc]], out=sq[:, cc, :], in_=pt[:], func=AF.Square)
            rms_rstd(S, [sq[:, kc, :] for kc in range(8)], sqb, ones[:], pss[:], pssb, rs[:], rsb, rstd[:], rstdb, epsT[:, 0:1], 1.0 / D, const_bufs=(onesb, epsb))
            for kc in range(8):
                A(S, "dve", "scalar_tensor_tensor", reads=[osbb[kc], gcb, rstdb], writes=[osbb[kc]], out=osb[:, kc, :], in0=osb[:, kc, :],
                  scalar=gcol[:, kc:kc + 1], in1=rstd[:], op0=ALU.mult, op1=ALU.mult)
                A(S, "pool", "tensor_tensor", reads=[osbb[kc], xb[s]], writes=[osbb[kc]], out=osb[:, kc, :], in0=osb[:, kc, :], in1=xt[s][:, kc, :], op=ALU.add)
            store(S, fm(outT, 0, 8, t0), osb[:], osbb)
        S.flush()
TWO_PI = 2.0 * math.pi


def sincos(S, x, xb, tmp, tmpb, out_s, out_sb, out_c, out_cb, pib):
    MAGIC = 12582912.0
    A(S, "dve", "tensor_scalar", reads=[xb], writes=[tmpb], out=tmp, in0=x, scalar1=MAGIC, scalar2=None, op0=ALU.add)
    A(S, "dve", "tensor_scalar", reads=[tmpb], writes=[tmpb], out=tmp, in0=tmp, scalar1=-MAGIC, scalar2=None, op0=ALU.add)
    A(S, "dve", "tensor_tensor", reads=[xb, tmpb], writes=[tmpb], out=tmp, in0=x, in1=tmp, op=ALU.subtract)
    A(S, "act", "activation", reads=[tmpb], writes=[out_sb], out=out_s, in_=tmp, func=AF.Sin, scale=TWO_PI)
    A(S, "act", "activation", reads=[tmpb], writes=[out_cb], out=out_c, in_=tmp, func=AF.Sin, scale=math.pi)
    A(S, "dve", "tensor_tensor", reads=[out_cb], writes=[out_cb], out=out_c, in0=out_c, in1=out_c, op=ALU.mult)
    A(S, "dve", "tensor_scalar", reads=[out_cb], writes=[out_cb], out=out_c, in0=out_c, scalar1=-2.0, scalar2=1.0, op0=ALU.mult, op1=ALU.add)


def phase_ssm_setup(S, L, dr, inp):
    N = 2048
    with ExitStack() as es:
        S.es = es
        rowp = S.sb("rowp", [128, 3, N], F32); rowb = Buf("rowp")
        colp = S.sb("colp", [128, 3, 16], F32); colb = Buf("colp")
        Bst = S.sb("Bst", [128, 2, N], F32); Bstb = Buf("Bst")
        iota = S.sb("iota", [128, 513], F32); iotab = Buf("iota")
        piT = S.sb("piT", [128, 1], F32); pitb = Buf("piT")
        names = ["dt", "ar", "th", "r", "f", "fc", "sn", "cs", "lbr", "lbi", "den", "kr", "ki", "ta", "tb"]
        tm = {n: S.sb("s_" + n, [128, N], F32) for n in names}
        tb = {n: Buf("s_" + n) for n in names}
        wb = S.sb("wbo", [128, 2, N], BF16); wbb = bufs("wbo", 2)
        cdt = S.sb("cdt", [128, 16], F32); cdtb = Buf("cdt")
        car = S.sb("car", [128, 16], F32); carb = Buf("car")
        cth = S.sb("cth", [128, 16], F32); cthb = Buf("cth")
        rcol = S.sb("rcol", [128, 16], F32); rcolb = Buf("rcol")
        ff = S.sb("ff", [128, 2, 513], F32); ffb = bufs("ff", 2)
        ft = S.sb("ft", [128, 2, 513], F32); ftb = bufs("ft", 2)
        tabs = S.sb("tabs", [128, 2, 2, 513], F32); tabsb = [bufs(f"tabs{i}_", 2) for i in range(2)]

        A(S, "dve", "memset", writes=[pitb], ap=piT[:], constant=math.pi)
        load(S, rowp[:], inp["ssm_rows"][L], rowb)
        load(S, colp[:], inp["ssm_cols"][L], colb)
        load(S, Bst[:], inp["ssm_bblk"][L], Bstb)
        load(S, iota[:], inp["iota"], iotab)
        are, aim, ldt = rowp[:, 0, :], rowp[:, 1, :], rowp[:, 2, :]

        def T_(eng, _opn, o, ins, **kw):
            A(S, eng, _opn, reads=[rowb] + [tb[i] for i in ins if i in tb], writes=[tb[o]], **kw)

        T_("act", "activation", "dt", [], out=tm["dt"][:], in_=ldt, func=AF.Exp)
        T_("dve", "tensor_tensor", "ar", ["dt"], out=tm["ar"][:], in0=are, in1=tm["dt"][:], op=ALU.mult)
        T_("dve", "tensor_tensor", "th", ["dt"], out=tm["th"][:], in0=aim, in1=tm["dt"][:], op=ALU.mult)
        T_("act", "activation", "r", ["ar"], out=tm["r"][:], in_=tm["ar"][:], func=AF.Exp)
        T_("dve", "tensor_scalar", "f", ["th"], out=tm["f"][:], in0=tm["th"][:], scalar1=1.0 / TWO_PI, scalar2=None, op0=ALU.mult)
        sincos(S, tm["f"][:], tb["f"], tm["fc"][:], tb["fc"], tm["sn"][:], tb["sn"], tm["cs"][:], tb["cs"], (piT[:, 0:1], pitb))
        T_("dve", "tensor_tensor", "lbr", ["r", "cs"], out=tm["lbr"][:], in0=tm["r"][:], in1=tm["cs"][:], op=ALU.mult)
        T_("dve", "tensor_tensor", "lbi", ["r", "sn"], out=tm["lbi"][:], in0=tm["r"][:], in1=tm["sn"][:], op=ALU.mult)
        T_("dve", "tensor_scalar", "lbr", ["lbr"], out=tm["lbr"][:], in0=tm["lbr"][:], scalar1=-1.0, scalar2=None, op0=ALU.add)
        T_("dve", "tensor_tensor", "ta", [], out=tm["ta"][:], in0=are, in1=are, op=ALU.mult)
        T_("dve", "tensor_tensor", "tb", [], out=tm["tb"][:], in0=aim, in1=aim, op=ALU.mult)
        T_("dve", "tensor_tensor", "den", ["ta", "tb"], out=tm["den"][:], in0=tm["ta"][:], in1=tm["tb"][:], op=ALU.add)
        T_("dve", "reciprocal", "den", ["den"], out=tm["den"][:], in_=tm["den"][:])
        T_("dve", "tensor_tensor", "ta", ["lbr"], out=tm["ta"][:], in0=tm["lbr"][:], in1=are, op=ALU.mult)
        T_("dve", "tensor_tensor", "tb", ["lbi"], out=tm["tb"][:], in0=tm["lbi"][:], in1=aim, op=ALU.mult)
        T_("dve", "tensor_tensor", "kr", ["ta", "tb"], out=tm["kr"][:], in0=tm["ta"][:], in1=tm["tb"][:], op=ALU.add)
        T_("dve", "tensor_tensor", "kr", ["kr", "den"], out=tm["kr"][:], in0=tm["kr"][:], in1=tm["den"][:], op=ALU.mult)
        T_("dve", "tensor_tensor", "ta", ["lbi"], out=tm["ta"][:], in0=tm["lbi"][:], in1=are, op=ALU.mult)
        T_("dve", "tensor_tensor", "tb", ["lbr"], out=tm["tb"][:], in0=tm["lbr"][:], in1=aim, op=ALU.mult)
        T_("dve", "tensor_tensor", "ki", ["ta", "tb"], out=tm["ki"][:], in0=tm["ta"][:], in1=tm["tb"][:], op=ALU.subtract)
        T_("dve", "tensor_tensor", "ki", ["ki", "den"], out=tm["ki"][:], in0=tm["ki"][:], in1=tm["den"][:], op=ALU.mult)
        Br, Bi = Bst[:, 0, :], Bst[:, 1, :]
        A(S, "dve", "tensor_tensor", reads=[tb["kr"], Bstb], writes=[tb["ta"]], out=tm["ta"][:], in0=tm["kr"][:], in1=Br, op=ALU.mult)
        A(S, "dve", "tensor_tensor", reads=[tb["ki"], Bstb], writes=[tb["tb"]], out=tm["tb"][:], in0=tm["ki"][:], in1=Bi, op=ALU.mult)
        A(S, "dve", "tensor_tensor", reads=[tb["ta"], tb["tb"]], writes=[wbb[0]], out=wb[:, 0, :], in0=tm["ta"][:], in1=tm["tb"][:], op=ALU.subtract)
        A(S, "dve", "tensor_tensor", reads=[tb["kr"], Bstb], writes=[tb["ta"]], out=tm["ta"][:], in0=tm["kr"][:], in1=Bi, op=ALU.mult)
        A(S, "dve", "tensor_tensor", reads=[tb["ki"], Bstb], writes=[tb["tb"]], out=tm["tb"][:], in0=tm["ki"][:], in1=Br, op=ALU.mult)
        A(S, "dve", "tensor_tensor", reads=[tb["ta"], tb["tb"]], writes=[wbb[1]], out=wb[:, 1, :], in0=tm["ta"][:], in1=tm["tb"][:], op=ALU.add)
        store(S, dr["ssm_wb"], wb[:], wbb)
        A(S, "act", "activation", reads=[colb], writes=[cdtb], out=cdt[:], in_=colp[:, 2, :], func=AF.Exp)
        A(S, "dve", "tensor_tensor", reads=[colb, cdtb], writes=[carb], out=car[:], in0=colp[:, 0, :], in1=cdt[:], op=ALU.mult)
        A(S, "act", "activation", reads=[carb], writes=[rcolb], out=rcol[:], in_=car[:], func=AF.Exp)
        store(S, dr["ssm_rcol"], rcol[:], rcolb)
        A(S, "dve", "tensor_tensor", reads=[colb, cdtb], writes=[cthb], out=cth[:], in0=colp[:, 1, :], in1=cdt[:], op=ALU.mult)
        A(S, "dve", "tensor_scalar", reads=[cthb], writes=[cthb], out=cth[:], in0=cth[:], scalar1=1.0 / TWO_PI, scalar2=None, op0=ALU.mult)
        for i in range(16):
            s = i % 2
            A(S, "dve", "tensor_scalar", reads=[iotab, cthb], writes=[ffb[s]], out=ff[:, s, :], in0=iota[:], scalar1=cth[:, i:i + 1], scalar2=None,
              op0=ALU.mult)
            sincos(S, ff[:, s, :], ffb[s], ft[:, s, :], ftb[s], tabs[:, s, 1, :], tabsb[s][1], tabs[:, s, 0, :], tabsb[s][0], (piT[:, 0:1], pitb))
            store(S, dr["ssm_tab"][:, i, :, :], tabs[:, s, :, :], tabsb[s])
        S.flush()


def phase_ssm(S, L, dr, inp, ntiles=NT):
    with ExitStack() as es:
        S.es = es
        tab = S.sb("tab", [128, 16, 2, 513], F32); tabb = Buf("tab")
        WB = S.sb("WB", [128, 2, 2048], BF16); WBb = Buf("WB")
        WC = S.sb("WC", [128, 2, 2048], BF16); WCb = Buf("WC")
        WCn = S.sb("WCn", [128, 2048], BF16); WCnb = Buf("WCn")
        Wg = S.sb("Wg", [128, 4, 1024], BF16); Wgb = Buf("Wg")
        Wo = S.sb("Wo", [128, 4, 1024], BF16); Wob = Buf("Wo")
        rcol = S.sb("rcol", [128, 16], F32); rcolb = Buf("rcol")
        dcol = S.sb("dcol", [128, 4], F32); dcolb = Buf("dcol")
        rot = S.sb("rot", [128, 16, 3], F32); rotb = Buf("rot")
        car = S.sb("car", [128, 16, 2], F32); carb = bufs("car", 16)
        ab = S.sb("ab", [128, 2, 2], F32); abb = bufs("ab", 2)
        ut = [S.sb(f"u{i}", [128, 4, TT], BF16) for i in range(2)]; utb = bufs("u", 2)
        NW = 2
        wk = {n: S.sb("w_" + n, [128, NW, TT], F32) for n in ["t1", "t2", "t3", "t4", "mr", "mi", "wr", "wi", "t5", "t6", "t7", "t8"]}
        wkb = {n: bufs("w_" + n, NW) for n in wk}
        xr = S.sb("xr", [128, NW, TT], BF16); xrb = bufs("xr", NW)
        xi = S.sb("xi", [128, NW, TT], BF16); xib = bufs("xi", NW)
        ysk = S.sb("ysk", [128, 2, TT], F32); yskb = bufs("ysk", 2)
        x2 = S.sb("x2", [128, 2, TT], F32); x2b = bufs("x2", 2)
        yg = S.sb("yg", [128, 4, TT], BF16); ygb = bufs("yg", 4)
        sl = S.sb("sl", [128, 2, TT], BF16); slb = bufs("sl", 2)
        sgl = S.sb("sgl", [128, 4, TT], BF16); sglb = bufs("sgl", 4)
        yo = S.sb("yo", [128, 8, TT], BF16); yob = bufs("yo", 8)
        pv = PsRot(S, 4, "pv")
        py = PsRot(S, 2, "py")
        pg = PsRot(S, 2, "pg")

        load(S, tab[:], dr["ssm_tab"], tabb)
        load(S, WB[:], dr["ssm_wb"], WBb)
        load(S, rcol[:], dr["ssm_rcol"], rcolb)
        load(S, dcol[:], inp["ssm_dcol"][L], dcolb)
        for j in range(2):
            A(S, "pool", "dma_start", writes=[WCb], dma=True, out=WC[:, j, :], in_=inp["ssm_cblk"][L][:, j, :])
        load_w_cast(S, Wg, Wgb, inp["w_ssm_glu"][L], 4, 1024, maxc=1024)
        load_w_cast(S, Wo, Wob, inp["w_ssm_out"][L], 4, 1024, maxc=1024)
        A(S, "act", "mul", reads=[WCb], writes=[WCnb], out=WCn[:], in_=WC[:, 1, :], mul=-1.0)
        A(S, "dve", "tensor_copy", reads=[tabb], writes=[rotb], out=rot[:, :, 0:2], in_=tab[:, :, :, 512])
        A(S, "dve", "tensor_scalar", reads=[tabb, rotb], writes=[rotb], out=rot[:, :, 2], in0=tab[:, :, 1, 512], scalar1=-1.0, scalar2=None, op0=ALU.mult)
        for i in range(16):
            A(S, "dve", "memset", writes=[carb[i]], ap=car[:, i, :], constant=0.0)
        load(S, ut[0][:], fm(dr["uT"], 0, 4, 0), utb[0])
        for tt in range(ntiles):
            s = tt % 2
            t0 = tt * TT
            if tt + 1 < ntiles:
                load(S, ut[1 - s][:], fm(dr["uT"], 0, 4, t0 + TT), utb[1 - s])
            ypd = {}

            def tile_gen(i):
                q = i // 4
                w = i % NW
                C_ = tab[:, i, 0, 0:TT]
                S_ = tab[:, i, 1, 0:TT]
                vr, vrb = pv.next()
                vi, vib = pv.next()
                A(S, "pe", "matmul", reads=[WBb, utb[s]], writes=[vrb], out=vr[:], lhsT=WB[:, 0, i * 128:(i + 1) * 128], rhs=ut[s][:, q, :], start=True, stop=True)
                yield
                A(S, "pe", "matmul", reads=[WBb, utb[s]], writes=[vib], out=vi[:], lhsT=WB[:, 1, i * 128:(i + 1) * 128], rhs=ut[s][:, q, :], start=True, stop=True)
                yield

                def W_(n):
                    return wk[n][:, w, :]

                def dv(eng, o, in0, in1, op, rd):
                    A(S, eng, "tensor_tensor", reads=rd, writes=[wkb[o][w]], out=W_(o), in0=in0, in1=in1, op=op)

                dv("dve", "t1", vr[:], C_, ALU.mult, [vrb, tabb])
                yield
                dv("dve", "t2", vi[:], S_, ALU.mult, [vib, tabb])
                yield
                dv("dve", "mr", W_("t1"), W_("t2"), ALU.add, [wkb["t1"][w], wkb["t2"][w]])
                yield
                dv("dve", "t3", vi[:], C_, ALU.mult, [vib, tabb])
                yield
                dv("dve", "t4", vr[:], S_, ALU.mult, [vrb, tabb])
                yield
                dv("dve", "mi", W_("t3"), W_("t4"), ALU.subtract, [wkb["t3"][w], wkb["t4"][w]])
                yield
                rb = rcol[:, i:i + 1].to_broadcast([128, TT])
                A(S, "dve", "tensor_tensor_scan", reads=[wkb["mr"][w], rcolb, carb[i]], writes=[wkb["wr"][w]], out=W_("wr"), data0=rb, data1=W_("mr"),
                  initial=car[:, i, 0:1], op0=ALU.mult, op1=ALU.add)
                yield
                A(S, "dve", "tensor_tensor_scan", reads=[wkb["mi"][w], rcolb, carb[i]], writes=[wkb["wi"][w]], out=W_("wi"), data0=rb, data1=W_("mi"),
                  initial=car[:, i, 1:2], op0=ALU.mult, op1=ALU.add)
                yield
                wrl = wk["wr"][:, w, TT - 1:TT]
                wil = wk["wi"][:, w, TT - 1:TT]
                k = i % 2
                A(S, "dve", "tensor_tensor", reads=[wkb["wr"][w], rotb], writes=[abb[k]], out=ab[:, k, 0:1], in0=wrl, in1=rot[:, i, 0:1], op=ALU.mult)
                yield
                A(S, "dve", "tensor_tensor", reads=[wkb["wr"][w], rotb, abb[k]], writes=[abb[k]], out=ab[:, k, 1:2], in0=wrl, in1=rot[:, i, 1:2], op=ALU.mult)
                yield
                A(S, "dve", "scalar_tensor_tensor", reads=[wkb["wi"][w], rotb, abb[k]], writes=[carb[i]], out=car[:, i, 0:1], in0=wil, scalar=rot[:, i, 2:3],
                  in1=ab[:, k, 0:1], op0=ALU.mult, op1=ALU.add)
                yield
                A(S, "dve", "scalar_tensor_tensor", reads=[wkb["wi"][w], rotb, abb[k], carb[i]], writes=[carb[i]], out=car[:, i, 1:2], in0=wil, scalar=rot[:, i, 0:1],
                  in1=ab[:, k, 1:2], op0=ALU.mult, op1=ALU.add)
                yield
                dv("pool", "t5", W_("wr"), C_, ALU.mult, [wkb["wr"][w], tabb])
                yield
                dv("pool", "t6", W_("wi"), S_, ALU.mult, [wkb["wi"][w], tabb])
                yield
                A(S, "pool", "tensor_tensor", reads=[wkb["t5"][w], wkb["t6"][w]], writes=[xrb[w]], out=xr[:, w, :], in0=W_("t5"), in1=W_("t6"), op=ALU.subtract)
                yield
                dv("pool", "t7", W_("wr"), S_, ALU.mult, [wkb["wr"][w], tabb])
                yield
                dv("pool", "t8", W_("wi"), C_, ALU.mult, [wkb["wi"][w], tabb])
                yield
                A(S, "pool", "tensor_tensor", reads=[wkb["t7"][w], wkb["t8"][w]], writes=[xib[w]], out=xi[:, w, :], in0=W_("t7"), in1=W_("t8"), op=ALU.add)
                yield
                if i % 4 == 0:
                    ypd[q] = py.next()
                ypt, ypb = ypd[q]
                A(S, "pe", "matmul", reads=[WCb, xrb[w]], writes=[ypb], out=ypt[:], lhsT=WC[:, 0, i * 128:(i + 1) * 128], rhs=xr[:, w, :], start=(i % 4 == 0), stop=False)
                yield
                A(S, "pe", "matmul", reads=[WCnb, xib[w]], writes=[ypb], out=ypt[:], lhsT=WCn[:, i * 128:(i + 1) * 128], rhs=xi[:, w, :], start=False, stop=(i % 4 == 3))
                yield
                if i % 4 == 3:
                    e = q % 2
                    A(S, "dve", "scalar_tensor_tensor", reads=[utb[s], dcolb, ypb], writes=[yskb[e]], out=ysk[:, e, :], in0=ut[s][:, q, :], scalar=dcol[:, q:q + 1],
                      in1=ypt[:], op0=ALU.mult, op1=ALU.add)
                    yield
                    A(S, "act", "activation", reads=[yskb[e]], writes=[x2b[e]], out=x2[:, e, :], in_=ysk[:, e, :], func=AF.Square)
                    yield
                    A(S, "dve", "tensor_scalar", reads=[x2b[e]], writes=[x2b[e]], out=x2[:, e, :], in0=x2[:, e, :], scalar1=0.044715, scalar2=1.0, op0=ALU.mult, op1=ALU.add)
                    yield
                    A(S, "dve", "tensor_tensor", reads=[x2b[e], yskb[e]], writes=[x2b[e]], out=x2[:, e, :], in0=x2[:, e, :], in1=ysk[:, e, :], op=ALU.mult)
                    yield
                    A(S, "act", "activation", reads=[x2b[e]], writes=[x2b[e]], out=x2[:, e, :], in_=x2[:, e, :], func=AF.Sigmoid, scale=1.5957691216057308)
                    yield
                    A(S, "dve", "tensor_tensor", reads=[x2b[e], yskb[e]], writes=[ygb[q]], out=yg[:, q, :], in0=x2[:, e, :], in1=ysk[:, e, :], op=ALU.mult)
                    yield

            gens = []
            for i0 in range(0, 16, 2):
                gens = [tile_gen(i0), tile_gen(i0 + 1)]
                while gens:
                    for g_ in list(gens):
                        try:
                            next(g_)
                        except StopIteration:
                            gens.remove(g_)
            for c in range(4):
                r = c % 2
                pt, pb = mm_group(S, pg, lambda kc: Wg[:, kc, 512 + c * 128:512 + (c + 1) * 128], lambda kc: yg[:, kc, :], 4, lambda kc: [Wgb, ygb[kc]])
                A(S, "act", "activation", reads=[pb], writes=[slb[r]], out=sl[:, r, :], in_=pt[:], func=AF.Sigmoid)
                pt2, pb2 = mm_group(S, pg, lambda kc: Wg[:, kc, c * 128:(c + 1) * 128], lambda kc: yg[:, kc, :], 4, lambda kc: [Wgb, ygb[kc]])
                A(S, "dve", "tensor_tensor", reads=[pb2, slb[r]], writes=[sglb[c]], out=sgl[:, c, :], in0=pt2[:], in1=sl[:, r, :], op=ALU.mult)
            for cc in range(8):
                pt, pb = mm_group(S, pg, lambda kc: Wo[:, kc, cc * 128:(cc + 1) * 128], lambda kc: sgl[:, kc, :], 4, lambda kc: [Wob, sglb[kc]])
                A(S, "act", "copy", reads=[pb], writes=[yob[cc]], out=yo[:, cc, :], in_=pt[:])
            store(S, fm(dr["yaT"], 0, 8, t0), yo[:], yob)
        S.flush()
def phase_attn(S, L, dr, inp, lam_init, nheads=8, nqt=NT):
    VW = 130
    with ExitStack() as es:
        S.es = es
        KT = [S.sb(f"KT{i}", [128, T], BF16) for i in range(2)]; KTb = bufs("KT", 2)
        VP = [S.sb(f"VP{i}", [128, 64, VW], BF16) for i in range(2)]; VPb = bufs("VP", 2)
        QT = [S.sb(f"QT{i}", [128, TT], BF16) for i in range(2)]; QTb = bufs("QT", 2)
        PT = [[S.sb(f"PT{m}_{i}", [128, TT], BF16) for i in range(2)] for m in range(2)]; PTb = [bufs(f"PT{m}_", 2) for m in range(2)]
        btab = S.sb("btab", [128, 16, 2, 128], F32); btabb = Buf("btab")
        bt = S.sb("bt", [128, 16, 2, 128], BF16); btb = Buf("bt")
        mask = S.sb("mask", [128, 128], F32); maskb = Buf("mask")
        c31 = S.sb("c31", [128, 16], F32); c31b = Buf("c31")
        identf = S.sb("identf", [128, 128], F32); identfb = Buf("identf")
        identb = S.sb("identb", [128, 128], BF16); identbb = Buf("identb")
        lamv = S.sb("lamv", [128, 4, 64], F32); lamvb = Buf("lamv")
        lp = S.sb("lp", [128, 2, 64], F32); lpb = Buf("lp")
        ls = S.sb("ls", [128, 4], F32); lsb = Buf("ls")
        gsub = S.sb("gsub", [128, 128], F32); gsubb = Buf("gsub")
        epsT = S.sb("eps", [128, 1], F32); epsb = Buf("eps")
        osb = S.sb("osb", [128, 2, 2, VW], F32); osbb = [bufs(f"osb{i}_", 2) for i in range(2)]
        rr = S.sb("rr", [128, 2, 4], F32); rrb = bufs("rr", 2)
        t2 = S.sb("t2", [128, 2, 128], F32); t2b = bufs("t2", 2)
        o = S.sb("o", [128, 2, 128], F32); ob = bufs("o", 2)
        junk = S.sb("junk", [128, 2, 128], F32); junkb = bufs("junk", 2)
        on = S.sb("on", [128, 2, 128], BF16); onb = bufs("on", 2)
        ao = [S.sb(f"ao{i}", [128, TT], BF16) for i in range(2)]; aob = [bufs(f"ao{i}_", 4) for i in range(2)]
        pS = [[S.ps(f"pS{m}_{i}") for i in range(2)] for m in range(2)]; pSb = [bufs(f"pS{m}_", 2) for m in range(2)]
        pA = [S.ps(f"pA{i}") for i in range(3)]; pAb = bufs("pA", 8)
        pT = S.ps("pT", (128, 128), BF16); pTb = Buf("pT")

        def acc(m, qs):
            a = m * 4 + qs
            return pA[a // 3][:, (a % 3) * 129:(a % 3) * 129 + 129], pAb[a]

        A(S, "dve", "memset", writes=[epsb], ap=epsT[:], constant=1e-5)
        load(S, btab[:], inp["att_btab"], btabb)
        load(S, mask[:], inp["att_mask"], maskb)
        load(S, c31[:], inp["att_c31"], c31b)
        load(S, identf[:], inp["ident"], identfb)
        load(S, lamv[:], inp["att_lamv"][L], lamvb)
        load(S, gsub[:], inp["att_gsub"][L], gsubb)
        A(S, "dve", "tensor_copy", reads=[identfb], writes=[identbb], out=identb[:], in_=identf[:])
        A(S, "act", "mul", reads=[gsubb], writes=[gsubb], out=gsub[:], in_=gsub[:], mul=(1.0 - lam_init))
        A(S, "dve", "tensor_tensor", reads=[lamvb], writes=[lpb], out=lp[:, 0, :], in0=lamv[:, 0, :], in1=lamv[:, 1, :], op=ALU.mult)
        A(S, "dve", "tensor_tensor", reads=[lamvb, lpb], writes=[lpb], out=lp[:, 1, :], in0=lamv[:, 2, :], in1=lamv[:, 3, :], op=ALU.mult)
        A(S, "dve", "tensor_reduce", reads=[lpb], writes=[lsb], out=ls[:, 0:2], in_=lp[:], axis=mybir.AxisListType.X, op=ALU.add)
        A(S, "act", "activation", reads=[lsb], writes=[lsb], out=ls[:, 0:2], in_=ls[:, 0:2], func=AF.Exp)
        A(S, "dve", "tensor_tensor", reads=[lsb], writes=[lsb], out=ls[:, 2:3], in0=ls[:, 0:1], in1=ls[:, 1:2], op=ALU.subtract)
        A(S, "dve", "tensor_scalar", reads=[lsb], writes=[lsb], out=ls[:, 3:4], in0=ls[:, 2:3], scalar1=float(lam_init), scalar2=None, op0=ALU.add)
        lamc = ls[:, 3:4]
        for mm in range(16):
            A(S, "dve", "tensor_scalar", reads=[btabb, c31b], writes=[btabb], out=btab[:, mm, :, :], in0=btab[:, mm, :, :], scalar1=c31[:, mm:mm + 1], scalar2=None,
              op0=ALU.subtract)
            A(S, "dve", "tensor_tensor", reads=[btabb, maskb], writes=[btabb], out=btab[:, mm, 0, :], in0=btab[:, mm, 0, :], in1=mask[:], op=ALU.add)
        A(S, "dve", "tensor_copy", reads=[btabb], writes=[btb], out=bt[:], in_=btab[:])
        for i in range(2):
            A(S, "dve", "memset", writes=[VPb[i]], ap=VP[i][:, :, 128:VW], constant=1.0)

        vv = dr["v"].rearrange("(j p) e -> p j e", p=128)

        def load_head(h):
            hs = h % 2
            load(S, KT[hs][:], dr["kT"][128 * h:128 * h + 128, :], KTb[hs])
            load(S, VP[hs][:, :, 0:128], vv[:, :, 128 * h:128 * h + 128], VPb[hs])

        def load_q(h, I, sl_):
            load(S, QT[sl_][:], dr["qT"][128 * h:128 * h + 128, I * TT:(I + 1) * TT], QTb[sl_])

        load_head(0)
        qcnt = 0
        load_q(0, 0, 0)
        jc = 0
        fcnt = 0
        for h in range(nheads):
            hs = h % 2
            if h + 1 < nheads:
                load_head(h + 1)
            for I in range(nqt):
                qsl = qcnt % 2
                qcnt += 1
                if I + 1 < nqt:
                    load_q(h, I + 1, 1 - qsl)
                elif h + 1 < nheads:
                    load_q(h + 1, 0, 1 - qsl)
                nkb = 4 * I + 4
                for bk in range(3):
                    A(S, "dve", "memset", writes=[pAb[a_] for a_ in range(3 * bk, min(8, 3 * bk + 3))], ap=pA[bk][:, 0:387], constant=0.0)
                slots = {}

                def emit_S(j):
                    nonlocal jc
                    a = j - 4 * I
                    qmin = max(0, a)
                    c0 = 128 * qmin
                    sl_ = jc % 2
                    jc += 1
                    slots[j] = sl_
                    for m in range(2):
                        mp = 2 * h + m
                        A(S, "pe", "matmul", reads=[KTb[hs], QTb[qsl]], writes=[pSb[m][sl_]], out=pS[m][sl_][:, c0:TT],
                          lhsT=KT[hs][64 * m:64 * m + 64, 128 * j:128 * j + 128], rhs=QT[qsl][64 * m:64 * m + 64, c0:TT], start=True, stop=True,
                          skip_group_check=True)
                        for qs in range(qmin, 4):
                            dl = 4 * I + qs - j
                            if dl in (0, 1):
                                A(S, "pe", "matmul", reads=[identbb, btb], writes=[pSb[m][sl_]], out=pS[m][sl_][:, 128 * qs:128 * qs + 128],
                                  lhsT=identb[:], rhs=bt[:, mp, dl, :], start=False, stop=True, skip_group_check=True)
                        A(S, "act", "activation", reads=[pSb[m][sl_], c31b], writes=[PTb[m][sl_]], out=PT[m][sl_][:, c0:TT], in_=pS[m][sl_][:, c0:TT],
                          func=AF.Exp, bias=c31[:, mp:mp + 1], scale=1.0)

                def emit_PV(j):
                    qmin = max(0, j - 4 * I)
                    sl_ = slots[j]
                    for m in range(2):
                        for qs in range(qmin, 4):
                            ap_, ab_ = acc(m, qs)
                            A(S, "pe", "matmul", reads=[PTb[m][sl_], VPb[hs]], writes=[ab_], out=ap_, lhsT=PT[m][sl_][:, 128 * qs:128 * qs + 128],
                              rhs=VP[hs][:, j, 0:129], start=False, stop=(j == 4 * I + qs), skip_group_check=True)

                emit_S(0)
                for j in range(nkb):
                    if j + 1 < nkb:
                        emit_S(j + 1)
                    emit_PV(j)
                asl = (h * nqt + I) % 2
                for qs in range(4):
                    f = fcnt % 2
                    fcnt += 1
                    a0, a0b = acc(0, qs)
                    a1, a1b = acc(1, qs)
                    A(S, "act", "copy", reads=[a0b], writes=[osbb[f][0]], out=osb[:, f, 0, 0:129], in_=a0)
                    A(S, "dve", "tensor_copy", reads=[a1b], writes=[osbb[f][1]], out=osb[:, f, 1, 0:129], in_=a1)
                    A(S, "dve", "reciprocal", reads=osbb[f], writes=[rrb[f]], out=rr[:, f, 0:2], in_=osb[:, f, :, 128])
                    A(S, "dve", "tensor_tensor", reads=[rrb[f], lsb], writes=[rrb[f]], out=rr[:, f, 2:3], in0=rr[:, f, 1:2], in1=lamc, op=ALU.mult)
                    A(S, "dve", "tensor_scalar", reads=[osbb[f][1], rrb[f]], writes=[t2b[f]], out=t2[:, f, :], in0=osb[:, f, 1, 0:128], scalar1=rr[:, f, 2:3], scalar2=None,
                      op0=ALU.mult)
                    A(S, "dve", "scalar_tensor_tensor", reads=[osbb[f][0], rrb[f], t2b[f]], writes=[ob[f]], out=o[:, f, :], in0=osb[:, f, 0, 0:128], scalar=rr[:, f, 0:1],
                      in1=t2[:, f, :], op0=ALU.mult, op1=ALU.subtract)
                    A(S, "act", "activation", reads=[ob[f]], writes=[junkb[f], rrb[f]], out=junk[:, f, :], in_=o[:, f, :], func=AF.Square, accum_out=rr[:, f, 3:4])
                    A(S, "act", "activation", reads=[rrb[f], epsb], writes=[rrb[f]], out=rr[:, f, 3:4], in_=rr[:, f, 3:4], func=AF.Sqrt, bias=epsT[:, 0:1], scale=1.0 / 128)
                    A(S, "dve", "reciprocal", reads=[rrb[f]], writes=[rrb[f]], out=rr[:, f, 3:4], in_=rr[:, f, 3:4])
                    A(S, "dve", "scalar_tensor_tensor", reads=[ob[f], rrb[f], gsubb], writes=[onb[f]], out=on[:, f, :], in0=o[:, f, :], scalar=rr[:, f, 3:4],
                      in1=gsub[:], op0=ALU.mult, op1=ALU.mult)
                    A(S, "pe", "transpose", reads=[onb[f], identbb], writes=[pTb], out=pT[:], in_=on[:, f, :], identity=identb[:])
                    A(S, "dve", "tensor_copy", reads=[pTb], writes=[aob[asl][qs]], out=ao[asl][:, 128 * qs:128 * qs + 128], in_=pT[:])
                store(S, dr["aT"][128 * h:128 * h + 128, I * TT:(I + 1) * TT], ao[asl][:], aob[asl])
        S.flush()


def make_dram(nc, debug):
    kind = "ExternalOutput" if debug else "Internal"
    dr = {}

    def mk(name, shape, dt=BF16):
        dr[name] = nc.dram_tensor(name, list(shape), dt, kind=kind).ap()

    mk("hT", [D, T]); mk("uT", [512, T]); mk("gT", [512, T]); mk("qT", [D, T]); mk("kT", [D, T])
    mk("v", [T, D]); mk("gatesT", [3 * D, T])
    mk("yaT", [D, T]); mk("ybT", [D, T]); mk("ycT", [D, T]); mk("aT", [D, T])
    mk("h2T", [D, T]); mk("fT", [DFF, T])
    mk("ssm_wb", [128, 2, 2048]); mk("ssm_rcol", [128, 16], F32); mk("ssm_tab", [128, 16, 2, 513], F32)
    return dr


ALL_PHASES = ["p1a", "p1b", "ssm_setup", "ssm", "conv", "attn", "attn_out", "merge", "ffn_in", "ffn_out"]


def build(shapes, n_layers=DEPTH, phases=None, debug=False, ntiles=NT, nheads=8):
    nc = bass.Bass("TRN2", target_bir_lowering=False)
    inp = {}
    for name, shp in shapes.items():
        inp[name] = nc.dram_tensor(name, list(shp), F32, kind="ExternalInput").ap()
    outT = nc.dram_tensor("outT", [D, T], F32, kind="ExternalOutput").ap()
    dr = make_dram(nc, debug)
    S = Sched(nc)
    for L in range(n_layers):
        xsrc = inp["xT"] if L == 0 else outT
        lam_init = 0.8 - 0.6 * math.exp(-0.3 * L)
        for ph in (phases or ALL_PHASES):
            if ph == "p1a":
                phase_p1a(S, L, xsrc, dr, inp, ntiles)
            elif ph == "p1b":
                phase_p1b(S, L, dr, inp, ntiles)
            elif ph == "ssm_setup":
                phase_ssm_setup(S, L, dr, inp)
            elif ph == "ssm":
                phase_ssm(S, L, dr, inp, ntiles)
            elif ph == "conv":
                phase_conv(S, L, dr, inp, ntiles)
            elif ph == "attn":
                phase_attn(S, L, dr, inp, lam_init, nheads, ntiles)
            elif ph == "attn_out":
                phase_linear(S, dr["aT"], dr["ycT"], inp["w_attn_out"][L], 8, 8, ntiles)
            elif ph == "merge":
                phase_merge(S, L, xsrc, outT, dr, inp, ntiles)
            elif ph == "ffn_in":
                phase_ffn_in(S, L, dr, inp, ntiles)
            elif ph == "ffn_out":
                phase_ffn_out(S, L, outT, dr, inp, ntiles)
    return nc


def _cols(v, n):
    Ld = v.shape[0]
    return np.ascontiguousarray(v.reshape(Ld, n, 128).transpose(0, 2, 1)).astype(np.float32)


def _rel_bucket(n):
    n = np.maximum(n, 0)
    nf = np.maximum(n, 16).astype(np.float32)
    large = 16 + (np.log(nf / np.float32(16)) / np.float32(math.log(128 / 16)) * np.float32(16)).astype(np.int32)
    large = np.minimum(large, 31)
    return np.where(n < 16, n, large)


def prep_shared(inputs):
    f32 = np.float32
    Ld = DEPTH
    sh = {}
    for k in ["w_in", "w_ssm_glu", "w_ssm_out", "w_conv_out", "w_attn_out", "w_out", "w_ffn_in", "w_ffn_out"]:
        sh[k] = np.ascontiguousarray(inputs[k], dtype=f32)
    sh["g_pre_mix"] = _cols(inputs["pre_mix_g"], 8)
    sh["g_post_mix"] = _cols(inputs["post_mix_g"], 8)
    sh["g_pre_ffn"] = _cols(inputs["pre_ffn_g"], 8)
    sh["g_post_ffn"] = _cols(inputs["post_ffn_g"], 8)
    dw = inputs["conv_dw"]
    sh["conv_dw_c"] = np.ascontiguousarray(dw.reshape(Ld, 31, 4, 128).transpose(0, 3, 2, 1)).astype(f32)
    prm = np.stack([_cols(inputs["conv_dw_b"], 4), _cols(inputs["conv_ln_g"], 4), _cols(inputs["conv_ln_b"], 4)], axis=2)
    sh["conv_prm"] = np.ascontiguousarray(prm).astype(f32)
    sh["ident"] = np.eye(128, dtype=f32)
    sh["iota"] = np.ascontiguousarray(np.broadcast_to(np.arange(513, dtype=f32), (128, 513)))
    are = inputs["ssm_a_re"].reshape(Ld, 2048); aim = inputs["ssm_a_im"].reshape(Ld, 2048)
    ldt = np.repeat(inputs["ssm_log_dt"], 64, axis=1).reshape(Ld, 2048)
    rows = np.stack([are, aim, ldt], axis=1)
    sh["ssm_rows"] = np.ascontiguousarray(np.broadcast_to(rows[:, None], (Ld, 128, 3, 2048))).astype(f32)
    sh["ssm_cols"] = np.ascontiguousarray(rows.reshape(Ld, 3, 16, 128).transpose(0, 3, 1, 2)).astype(f32)
    bblk = np.zeros((Ld, 128, 2, 16, 128), f32)
    cblk = np.zeros((Ld, 128, 2, 16, 128), f32)
    for g in range(32):
        i = g // 2
        po = 64 * (g % 2)
        co = 16 * (g % 8)
        for ri, (bn, cn) in enumerate((("ssm_b_re", "ssm_c_re"), ("ssm_b_im", "ssm_c_im"))):
            bblk[:, co:co + 16, ri, i, po:po + 64] = inputs[bn][:, g].transpose(0, 2, 1)
            cblk[:, po:po + 64, ri, i, co:co + 16] = inputs[cn][:, g].transpose(0, 2, 1)
    sh["ssm_bblk"] = bblk.reshape(Ld, 128, 2, 2048)
    sh["ssm_cblk"] = cblk.reshape(Ld, 128, 2, 2048)
    sh["ssm_dcol"] = _cols(inputs["ssm_d"].reshape(Ld, 512), 4)
    rb = np.asarray(inputs["rel_bias"], f32)
    kk = np.arange(128)[:, None]; qq = np.arange(128)[None, :]
    bt = np.zeros((128, 16, 2, 128), f32)
    for dl in range(2):
        bk = _rel_bucket(qq - kk + 128 * dl)
        bt[:, :, dl, :] = rb[bk].transpose(0, 2, 1)
    sh["att_btab"] = bt
    sh["att_mask"] = np.where(qq >= kk, 0.0, -30000.0).astype(f32)
    sh["att_c31"] = np.ascontiguousarray(np.broadcast_to(rb[31][None, :], (128, 16))).astype(f32)
    lv = np.stack([inputs["lambda_q1"], inputs["lambda_k1"], inputs["lambda_q2"], inputs["lambda_k2"]], axis=1)
    sh["att_lamv"] = np.ascontiguousarray(np.broadcast_to(lv[:, None], (Ld, 128, 4, 64))).astype(f32)
    sh["att_gsub"] = np.ascontiguousarray(np.broadcast_to(inputs["attn_subln_g"][:, None, :], (Ld, 128, 128))).astype(f32)
    return sh


_NC_CACHE = {}


def kernel(**inputs):
    inputs = {k: np.asarray(v) for k, v in inputs.items()}
    sh = prep_shared(inputs)
    x = inputs["x"]
    shapes = {"xT": (D, T)}
    shapes.update({k: v.shape for k, v in sh.items()})
    if "nc" not in _NC_CACHE:
        _NC_CACHE["nc"] = build(shapes)
    nc = _NC_CACHE["nc"]
    in_maps = []
    for c in range(8):
        m = dict(sh)
        m["xT"] = np.ascontiguousarray(x[c // 2].T)
        in_maps.append(m)
    res = run_bass_kernel_spmd(nc, in_maps, core_ids=list(range(8)))
    out = np.stack([np.ascontiguousarray(res.results[2 * b]["outT"].T) for b in range(4)], axis=0)
    return out.astype(np.float32)
```
